# Optimizing a Trainium2 kernel written in Bass

```python
import math
import jax, jax.numpy as jnp
from jax import lax
import numpy as np

D_MODEL = 1024
BATCH = 8
SEQ = 8192
DEPTH = 2

N_A_LAYERS = DEPTH // 2
N_B_LAYERS = DEPTH - N_A_LAYERS
CHUNK = 128
A_WIDTH = D_MODEL
A_GROUPS = 8
A_GROUP_DIM = A_WIDTH // A_GROUPS
HEAD_DIM = 64
N_Q_HEADS = D_MODEL // HEAD_DIM
N_KV_HEADS = 4
GQA_GROUP = N_Q_HEADS // N_KV_HEADS
WINDOW = 128
BLOCK = 128
D_FF = 2816
CONV_WIDTH = 3
PLE_DIM = 256
EPS = 1e-6

kernel_name = "yoco_gmlp_swa_sink_hybrid"


def _alibi_slopes(n):
    return np.array([2.0 ** (-8.0 * (h + 1) / n) for h in range(n)], dtype=np.float32)


def rmsnorm(x, g):
    xf = x.astype(jnp.float32)
    y = xf * lax.rsqrt(jnp.mean(xf * xf, axis=-1, keepdims=True) + EPS)
    return (y * g.astype(jnp.float32)).astype(x.dtype)


def mixer_a(xn, w_in, g_v, w_s, b_s, w_out):
    B, S, _ = xn.shape
    z = jax.nn.gelu(xn @ w_in)
    u, v = jnp.split(z, 2, axis=-1)
    v = rmsnorm(v, g_v)
    nc = S // CHUNK
    v = v.reshape(B, nc, CHUNK, A_GROUPS, A_GROUP_DIM)
    causal = jnp.tril(jnp.ones((CHUNK, CHUNK), dtype=bool))
    w = jnp.where(causal[None], w_s, jnp.zeros((), w_s.dtype)).astype(v.dtype)
    s = jnp.einsum('hts,bcshd->bcthd', w, v) + b_s.T.astype(v.dtype)[None, None, :, :, None]
    s = s.reshape(B, S, A_WIDTH)
    return (u * s) @ w_out


def shared_kv(h, g, w_kv):
    B, S, _ = h.shape
    nb = S // BLOCK
    kv = rmsnorm(h, g) @ w_kv
    k, v = jnp.split(kv, 2, axis=-1)

    def band(t):
        tb = t.reshape(B, nb, BLOCK, N_KV_HEADS, HEAD_DIM)
        prev = jnp.pad(tb[:, :-1], ((0, 0), (1, 0), (0, 0), (0, 0), (0, 0)))
        return jnp.concatenate([prev, tb], axis=2)

    return band(k), band(v)


def mixer_b(xn, w_q, sinks, w_o, kblk, vblk):
    B, S, _ = xn.shape
    nb = S // BLOCK
    q = (xn @ w_q).reshape(B, nb, BLOCK, N_KV_HEADS, GQA_GROUP, HEAD_DIM)
    scores = jnp.einsum('bnikgd,bnjkd->bnkgij', q.astype(jnp.float32),
                        kblk.astype(jnp.float32)) * (HEAD_DIM ** -0.5)
    i = jnp.arange(BLOCK)[:, None]
    j = jnp.arange(2 * BLOCK)[None, :]
    dist = i + BLOCK - j
    in_band = (dist >= 0) & (dist < WINDOW)
    blk = jnp.arange(nb)[:, None, None]
    valid = in_band[None] & ((blk > 0) | (j[None] >= BLOCK))
    slopes = jnp.asarray(_alibi_slopes(N_Q_HEADS)).reshape(N_KV_HEADS, GQA_GROUP)
    scores = scores - slopes[:, :, None, None] * dist.astype(jnp.float32)
    scores = jnp.where(valid[None, :, None, None], scores, -jnp.inf)
    sink = sinks.astype(jnp.float32).reshape(N_KV_HEADS, GQA_GROUP)[:, :, None, None]
    m = jnp.maximum(jnp.max(scores, axis=-1, keepdims=True), sink)
    e = jnp.exp(scores - m)
    probs = e / (jnp.sum(e, axis=-1, keepdims=True) + jnp.exp(sink - m))
    out = jnp.einsum('bnkgij,bnjkd->bnikgd', probs.astype(vblk.dtype), vblk)
    return out.reshape(B, S, N_Q_HEADS * HEAD_DIM) @ w_o


def conv_ffn(xn, w_up, conv_w, conv_b, w_down):
    S = xn.shape[1]
    h = xn @ w_up
    hp = jnp.pad(h, ((0, 0), (CONV_WIDTH - 1, 0), (0, 0)))
    c = conv_b + sum(hp[:, t:t + S] * conv_w[t] for t in range(CONV_WIDTH))
    g, u = jnp.split(c, 2, axis=-1)
    return (jax.nn.silu(g) * u) @ w_down


def per_layer_embed(h, p_i, g_norm, w_in, w_gate, b_gate):
    gate = jax.nn.sigmoid(rmsnorm(h, g_norm) @ w_gate + b_gate)
    return (p_i @ w_in) * gate


def setup_inputs(seed: int = 0) -> dict:
    key = jax.random.key(seed)
    ks = jax.random.split(key, 26)
    f32 = jnp.float32

    def nrm(k, shape, scale):
        return jax.random.normal(k, shape, f32) * scale

    def gain(k, shape):
        return 1.0 + 0.02 * jax.random.normal(k, shape, f32)

    d = D_MODEL
    return {
        "x": nrm(ks[0], (BATCH, SEQ, d), 1.0),
        "p": nrm(ks[1], (DEPTH, BATCH, SEQ, PLE_DIM), 1.0),
        "norm_mix": gain(ks[2], (DEPTH, d)),
        "norm_ffn": gain(ks[3], (DEPTH, d)),
        "norm_ple": gain(ks[4], (DEPTH, d)),
        "norm_kv": gain(ks[5], (d,)),
        "norm_final": gain(ks[6], (d,)),
        "a_w_in": nrm(ks[7], (N_A_LAYERS, d, 2 * A_WIDTH), d ** -0.5),
        "a_norm_v": gain(ks[8], (N_A_LAYERS, A_WIDTH)),
        "a_w_s": nrm(ks[9], (N_A_LAYERS, A_GROUPS, CHUNK, CHUNK), CHUNK ** -0.5),
        "a_b_s": 1.0 + nrm(ks[10], (N_A_LAYERS, A_GROUPS, CHUNK), 0.02),
        "a_w_out": nrm(ks[11], (N_A_LAYERS, A_WIDTH, d), A_WIDTH ** -0.5),
        "w_kv": nrm(ks[12], (d, 2 * N_KV_HEADS * HEAD_DIM), d ** -0.5),
        "b_w_q": nrm(ks[13], (N_B_LAYERS, d, N_Q_HEADS * HEAD_DIM), d ** -0.5),
        "b_sinks": nrm(ks[14], (N_B_LAYERS, N_Q_HEADS), 0.5),
        "b_w_o": nrm(ks[15], (N_B_LAYERS, N_Q_HEADS * HEAD_DIM, d), (N_Q_HEADS * HEAD_DIM) ** -0.5),
        "f_w_up": nrm(ks[16], (DEPTH, d, 2 * D_FF), d ** -0.5),
        "f_conv_w": nrm(ks[17], (DEPTH, CONV_WIDTH, 2 * D_FF), CONV_WIDTH ** -0.5),
        "f_conv_b": nrm(ks[18], (DEPTH, 2 * D_FF), 0.01),
        "f_w_down": nrm(ks[19], (DEPTH, D_FF, d), D_FF ** -0.5),
        "ple_w_in": nrm(ks[20], (DEPTH, PLE_DIM, d), PLE_DIM ** -0.5),
        "ple_w_gate": nrm(ks[21], (DEPTH, d, d), d ** -0.5),
        "ple_b_gate": nrm(ks[22], (DEPTH, d), 0.01),
    }


def reference(x, p, norm_mix, norm_ffn, norm_ple, norm_kv, norm_final,
              a_w_in, a_norm_v, a_w_s, a_b_s, a_w_out,
              w_kv, b_w_q, b_sinks, b_w_o,
              f_w_up, f_conv_w, f_conv_b, f_w_down,
              ple_w_in, ple_w_gate, ple_b_gate):
    h = x
    kblk = None
    vblk = None
    for i in range(DEPTH):
        xn = rmsnorm(h, norm_mix[i])
        if i < N_A_LAYERS:
            a = i
            h = h + mixer_a(xn, a_w_in[a], a_norm_v[a], a_w_s[a], a_b_s[a], a_w_out[a])
        else:
            b = i - N_A_LAYERS
            h = h + mixer_b(xn, b_w_q[b], b_sinks[b], b_w_o[b], kblk, vblk)
        h = h + conv_ffn(rmsnorm(h, norm_ffn[i]), f_w_up[i], f_conv_w[i], f_conv_b[i], f_w_down[i])
        h = h + per_layer_embed(h, p[i], norm_ple[i], ple_w_in[i], ple_w_gate[i], ple_b_gate[i])
        if i == N_A_LAYERS - 1:
            kblk, vblk = shared_kv(h, norm_kv, w_kv)
    return rmsnorm(h, norm_final)
```

```python
from concourse.bass_utils import run_bass_kernel_spmd

import numpy as np
import concourse.bass as bass
import concourse.mybir as mybir

F32 = mybir.dt.float32
BF16 = mybir.dt.bfloat16
I32 = mybir.dt.int32
AF = mybir.ActivationFunctionType
ALU = mybir.AluOpType

DT_SIZE = {F32: 4, BF16: 2, I32: 4}
STRICT_SAME_ENGINE = True


class Tile:
    def __init__(self, name, space, off, nbytes, ap, nparts=1):
        self.name = name
        self.space = space
        self.off = off
        self.nbytes = nbytes
        self.ap = ap
        self.nparts = nparts
        self.w = [dict() for _ in range(nparts)]
        self.r = [dict() for _ in range(nparts)]

    def __getitem__(self, key):
        return self.ap[key]

    def all_events(self):
        ev = {}
        for d in self.w + self.r:
            for k, v in d.items():
                if ev.get(k, 0) < v:
                    ev[k] = v
        return ev

    def inherit(self, ev):
        for p in range(self.nparts):
            for k, v in ev.items():
                if self.r[p].get(k, 0) < v:
                    self.r[p][k] = v


class Acc:
    def __init__(self, tile, parts=None):
        self.tile = tile
        self.parts = range(tile.nparts) if parts is None else parts


def _acc(x):
    if isinstance(x, Acc):
        return x
    if isinstance(x, Tile):
        return Acc(x)
    t, p = x
    if isinstance(p, int):
        p = [p]
    return Acc(t, p)


class Sched:
    COMPUTE = ("pe", "act", "dve", "pool")
    NDMA_SEMS = 12

    def __init__(self, nc, dma_queues=("sp", "act")):
        self.nc = nc
        self.h = {"pe": nc.tensor, "act": nc.scalar, "dve": nc.vector, "pool": nc.gpsimd, "sp": nc.sync}
        self.streams = {e: [] for e in self.h}
        self.sems = {}
        self.cnt = {}
        self.seen = {e: {} for e in self.h}
        for e in self.COMPUTE:
            self.sems[e] = nc.alloc_semaphore(name=f"s_{e}")
            self.cnt[e] = 0
        self.dma_rr = {}
        for q in dma_queues:
            self.dma_rr[q] = 0
            for i in range(self.NDMA_SEMS):
                k = f"d_{q}{i}"
                self.sems[k] = nc.alloc_semaphore(name=k)
                self.cnt[k] = 0
        self.n_ops = 0
        self.n_waits = 0

    def _deps(self, eng, reads, writes, strict=False):
        deps = {}

        def add(k, v, raw):
            if k == eng and not strict:
                if eng == "pe" or not (raw or STRICT_SAME_ENGINE):
                    return
            if deps.get(k, 0) < v:
                deps[k] = v

        for a in reads:
            for p in a.parts:
                for k, v in a.tile.w[p].items():
                    add(k, v, True)
        for a in writes:
            for p in a.parts:
                for k, v in a.tile.w[p].items():
                    add(k, v, False)
                for k, v in a.tile.r[p].items():
                    add(k, v, False)
        out = []
        for k, v in deps.items():
            if self.seen[eng].get(k, 0) < v:
                self.seen[eng][k] = v
                out.append((k, v))
        return out

    def _commit(self, key, val, reads, writes):
        for a in reads:
            for p in a.parts:
                d = a.tile.r[p]
                if d.get(key, 0) < val:
                    d[key] = val
        for a in writes:
            for p in a.parts:
                a.tile.w[p] = {key: val}
                a.tile.r[p] = {}

    def op(self, eng, fn, reads=(), writes=()):
        reads = [_acc(x) for x in reads]
        writes = [_acc(x) for x in writes]
        waits = self._deps(eng, reads, writes)
        self.cnt[eng] += 1
        val = self.cnt[eng]
        sem = self.sems[eng]
        wl = [(self.sems[k], v) for k, v in waits]
        self.n_ops += 1
        self.n_waits += len(wl)

        def thunk(hh, fn=fn, wl=wl, sem=sem):
            for s, v in wl:
                hh.wait_ge(s, v)
            fn(hh).then_inc(sem, 1)

        self.streams[eng].append(thunk)
        self._commit(eng, val, reads, writes)

    def op_multi(self, eng, fns, reads=(), writes=()):
        reads = [_acc(x) for x in reads]
        writes = [_acc(x) for x in writes]
        waits = self._deps(eng, reads, writes)
        self.cnt[eng] += 1
        val = self.cnt[eng]
        sem = self.sems[eng]
        wl = [(self.sems[k], v) for k, v in waits]
        self.n_ops += len(fns)
        self.n_waits += len(wl)

        def thunk(hh, fns=fns, wl=wl, sem=sem):
            for s, v in wl:
                hh.wait_ge(s, v)
            for f in fns[:-1]:
                f(hh)
            fns[-1](hh).then_inc(sem, 1)

        self.streams[eng].append(thunk)
        self._commit(eng, val, reads, writes)

    def dma(self, q, out_ap, in_ap, reads=(), writes=(), **kw):
        reads = [_acc(x) for x in reads]
        writes = [_acc(x) for x in writes]
        i = self.dma_rr[q]
        self.dma_rr[q] = (i + 1) % self.NDMA_SEMS
        key = f"d_{q}{i}"
        waits = self._deps(q, reads, writes, strict=True)
        prev = self.cnt[key]
        if prev > 0 and self.seen[q].get(key, 0) < prev:
            self.seen[q][key] = prev
            waits.append((key, prev))
        self.cnt[key] += 16
        val = self.cnt[key]
        sem = self.sems[key]
        wl = [(self.sems[k], v) for k, v in waits]
        self.n_ops += 1
        self.n_waits += len(wl)

        def thunk(hh, wl=wl, sem=sem, out_ap=out_ap, in_ap=in_ap, kw=kw):
            for s, v in wl:
                hh.wait_ge(s, v)
            hh.dma_start(out=out_ap, in_=in_ap, **kw).then_inc(sem, 16)

        self.streams[q].append(thunk)
        self._commit(key, val, reads, writes)
        return key, val

    def final_wait(self, eng, tiles):
        ev = {}
        for t in tiles:
            for k, v in t.all_events().items():
                if ev.get(k, 0) < v:
                    ev[k] = v
        wl = [(self.sems[k], v) for k, v in ev.items()]

        def thunk(hh, wl=wl):
            for s, v in wl:
                hh.wait_ge(s, v)

        self.streams[eng].append(thunk)

    def emit(self):
        nc = self.nc
        with nc.Block() as block:
            def mk(name):
                def f(hh):
                    for t in self.streams[name]:
                        t(hh)
                return f
            block.tensor(mk("pe"))
            block.scalar(mk("act"))
            block.vector(mk("dve"))
            block.gpsimd(mk("pool"))
            block.sync(mk("sp"))


class Mem:
    def __init__(self, nc, sbuf_bytes):
        self.nc = nc
        self.sb_raw = nc.alloc_sbuf_tensor("sb_raw", [128, sbuf_bytes // 4], F32)
        self.ps_raw = nc.alloc_psum_tensor("ps_raw", [128, 4096], F32)
        self.sbuf_bytes = sbuf_bytes
        self.live = {"sb": [], "ps": []}

    def _place(self, space, name, off, nbytes, dtype, shape, nparts, parts_total=128):
        raw = self.sb_raw if space == "sb" else self.ps_raw
        assert off % 4 == 0 and nbytes % 4 == 0, (name, off, nbytes)
        ap = raw[0:parts_total, off // 4:(off + nbytes) // 4]
        if dtype != F32:
            ap = ap.bitcast(dtype)
        if len(shape) == 2:
            ap = ap.rearrange("p (a b) -> p a b", a=shape[0], b=shape[1])
        elif len(shape) == 3:
            ap = ap.rearrange("p (a b c) -> p a b c", a=shape[0], b=shape[1], c=shape[2])
        t = Tile(name, space, off, nbytes, ap, nparts)
        keep = []
        for o in self.live[space]:
            if o.off < off + nbytes and off < o.off + o.nbytes:
                t.inherit(o.all_events())
                if not (off <= o.off and o.off + o.nbytes <= off + nbytes):
                    keep.append(o)
            else:
                keep.append(o)
        keep.append(t)
        self.live[space] = keep
        return t

    def sb(self, name, off, dtype, shape, nparts=1):
        n = int(np.prod(shape)) * DT_SIZE[dtype]
        assert off + n <= self.sbuf_bytes, (name, off, n, self.sbuf_bytes)
        return self._place("sb", name, off, n, dtype, list(shape), nparts)

    def ps(self, name, bank, dtype, shape, nparts=1, nbanks=1, byte_off=0):
        n = int(np.prod(shape)) * DT_SIZE[dtype]
        assert n + byte_off <= 2048 * nbanks
        return self._place("ps", name, bank * 2048 + byte_off, n, dtype, list(shape), nparts)


D = 1024
DFF = 2816
NM = 44
NMH = 22
KC = 8
PD = 256
HD = 64
EPS = 1e-6
SBUF_BYTES = 207 * 1024

INPUT_SHAPES = {
    "norm_mix": [2, D], "norm_ffn": [2, D], "norm_ple": [2, D], "norm_kv": [D], "norm_final": [D],
    "a_w_in": [1, D, 2 * D], "a_norm_v": [1, D], "a_w_s": [1, 8, 128, 128], "a_b_s": [1, 8, 128],
    "a_w_out": [1, D, D], "w_kv": [D, 512], "b_w_q": [1, D, D], "b_sinks": [1, 16], "b_w_o": [1, D, D],
    "f_w_up": [2, D, 2 * DFF], "f_conv_w": [2, 3, 2 * DFF], "f_conv_b": [2, 2 * DFF], "f_w_down": [2, DFF, D],
    "ple_w_in": [2, PD, D], "ple_w_gate": [2, D, D], "ple_b_gate": [2, D],
}


def hL(c):
    return 8 * (c // 4) + (c % 4)


def hU(c):
    return hL(c) + 4


class Bump:
    def __init__(self, base, size):
        self.base, self.size, self.p = base, size, base

    def take(self, n):
        n = (n + 31) // 32 * 32
        o = self.p
        self.p += n
        assert self.p <= self.base + self.size, ("sbuf overflow", self.p, self.base + self.size)
        return o


class Ring:
    def __init__(self, S, M, base, size, schedule):
        self.S, self.M, self.base, self.size = S, M, base, size
        self.schedule = schedule
        self.next = 0
        self.ptr = 0
        self.live = []
        self.head = 0

    def _try_place(self, n):
        unrel = [(o, nb) for (_, _, o, nb, rel) in self.live if not rel]
        cands = [self.ptr, 0] + sorted(o + nb for (o, nb) in unrel)
        for off in cands:
            if off + n > self.size:
                continue
            if all(not (o < off + n and off < o + nb) for (o, nb) in unrel):
                return off
        return None

    def prefetch(self, limit=None):
        cnt = 0
        while self.next < len(self.schedule):
            name, dt_, dap, shape = self.schedule[self.next]
            n = (int(np.prod(shape)) * 2 + 31) // 32 * 32
            off = self._try_place(n)
            if off is None:
                break
            self.live = [e for e in self.live if not (e[4] and e[2] < off + n and off < e[2] + e[3])]
            t = self.M.sb("w_" + str(name), self.base + off, BF16, shape)
            self.S.dma("sp", t.ap, dap, reads=[dt_], writes=[t])
            self.live.append([name, t, off, n, False])
            self.ptr = off + n
            self.next += 1
            cnt += 1
            if limit is not None and cnt >= limit:
                break

    def get(self, name):
        for e in self.live:
            if e[0] == name and not e[4]:
                return e[1]
        self.prefetch()
        for e in self.live:
            if e[0] == name and not e[4]:
                return e[1]
        raise RuntimeError(f"ring: piece {name} not loadable (ring too small?) next={self.schedule[self.next][0] if self.next < len(self.schedule) else None}")

    def release(self, name):
        for e in self.live:
            if e[0] == name and not e[4]:
                e[4] = True
                break
        else:
            raise RuntimeError(f"ring release: {name} not live")
        self.prefetch()


def build_program(S_core, T=512):
    NCH = T // 128
    NT = S_core // T
    assert NT * T == S_core
    nc = bass.Bass("TRN2", target_bir_lowering=False)

    def din(name, shape):
        return nc.dram_tensor(name, list(shape), F32, kind="ExternalInput").ap()

    x_d = din("x", [S_core, D])
    p_d = din("p", [2, S_core, PD])
    ident_d = din("ident", [128, 128])
    W = {k: din(k, v) for k, v in INPUT_SHAPES.items()}
    out_d = nc.dram_tensor("out", [S_core, D], F32, kind="ExternalOutput").ap()

    S = Sched(nc, dma_queues=("sp", "act", "pool"))
    M = Mem(nc, SBUF_BYTES)

    def dram_scratch(name, shape, nparts):
        ap = nc.dram_tensor(name, [128] + list(shape), BF16, kind="Internal").ap()
        return Tile(name, "dram", 0, 0, ap, nparts)

    s_win_u = dram_scratch("s_win_u", [4, 8, 256], 8)
    s_win_v = dram_scratch("s_win_v", [8, 1024], 8)
    s_wout = dram_scratch("s_wout", [8, 1024], 8)
    s_wup = [dram_scratch(f"s_wup{l}", [11, 8, 512], 16) for l in range(2)]
    s_wdown = [dram_scratch(f"s_wdown{l}", [22, 1024], 22) for l in range(2)]
    s_wgate = [dram_scratch(f"s_wgate{l}", [8, 1024], 8) for l in range(2)]
    s_plein = [dram_scratch(f"s_plein{l}", [2, 1024], 2) for l in range(2)]
    s_wkv = dram_scratch("s_wkv", [8, 512], 8)
    s_wq = dram_scratch("s_wq", [4, 8, 256], 8)
    s_wo = dram_scratch("s_wo", [8, 1024], 8)
    out_t = Tile("out", "dram", 0, 0, out_d, S_core // 128)

    A = Bump(0, SBUF_BYTES)

    def sbt(name, dtype, shape, nparts=1):
        n = int(np.prod(shape)) * DT_SIZE[dtype]
        return M.sb(name, A.take(n), dtype, shape, nparts)

    identf = sbt("identf", F32, [128])
    identb = sbt("identb", BF16, [128])
    ones = sbt("ones", BF16, [128])
    negh = sbt("negh", F32, [1])
    gcols = sbt("gcols", F32, [56])
    cw = sbt("cw", F32, [2, 4, NM])
    gv_tile = sbt("gv_tile", F32, [D])
    gfin_tile = sbt("gfin_tile", F32, [D])
    wsT = sbt("wsT", BF16, [8, 128])
    bs_hl = sbt("bs_hl", BF16, [2, D])
    bg_hl = sbt("bg_hl", BF16, [2, 2, D])
    sinkcol = sbt("sinkcol", F32, [2, 4])
    sinkt = sbt("sinkt", F32, [2, 512])
    Mk = sbt("Mk", BF16, [2, 4, 512])
    halo = [sbt(f"halo{l}", F32, [NM, 2]) for l in range(2)]
    corr = sbt("corr", F32, [NM, 2])
    ctmp = sbt("ctmp", F32, [2, NM])
    kT = sbt("kT", BF16, [2, 128 + T], nparts=NCH + 1)
    vpad = sbt("vpad", BF16, [NCH + 1, 4, 128], nparts=NCH + 1)
    ones_eo = sbt("ones_eo", BF16, [2, 128])
    stats = [[sbt(f"st{n}_{c}", F32, [4]) for c in range(NCH)] for n in range(7)]
    vstat = [sbt(f"vst{c}", F32, [4]) for c in range(NCH)]
    hA = sbt("hA", F32, [NCH, D], nparts=NCH)
    hB = sbt("hB", F32, [NCH, D], nparts=NCH)
    hs = [hA, hB]
    xn_a = sbt("xn_a", BF16, [KC, T], nparts=NCH)
    xn_b = sbt("xn_b", BF16, [KC, T], nparts=NCH)
    pT = [sbt(f"pT{l}", BF16, [2, T]) for l in range(2)]
    xn_tok = [sbt(f"xn_tok{i}", BF16, [D]) for i in range(3)]
    junk = sbt("junk", BF16, [D])

    scratch_base = A.take(0)
    SCRATCH = 38912
    A.take(SCRATCH)
    ring_base = A.take(0)
    ring_size = (SBUF_BYTES - ring_base) // 32 * 32

    class Phase:
        def __init__(self):
            self.b = Bump(scratch_base, SCRATCH)

        def t(self, name, dtype, shape, nparts=1):
            n = int(np.prod(shape)) * DT_SIZE[dtype]
            return M.sb(name, self.b.take(n), dtype, shape, nparts)

    psr = {"n": 0, "gen": 0, "S": 0, "W": 0, "OD": 0}
    ALLB = [0, 1, 2, 3, 4, 5, 6, 7]

    def ps_from(key, banks, name, dtype, shape, nbanks=1):
        i = psr[key] % len(banks)
        psr[key] += 1
        psr["n"] += 1
        return M.ps(f"{name}_{psr['n']}", banks[i], dtype, shape, nbanks=nbanks)

    def ps_narrow(name, dtype, shape):
        return ps_from("gen", ALLB, name, dtype, shape)

    def ps_wide(name, dtype, shape, nparts=1):
        if psr["gen"] % 2:
            psr["gen"] += 1
        b = psr["gen"] % 8
        psr["gen"] += 2
        psr["n"] += 1
        return M.ps(f"{name}_{psr['n']}", b, dtype, shape, nparts=nparts, nbanks=2)

    pro = Bump(scratch_base, SBUF_BYTES - scratch_base)

    def ptmp(name, dtype, shape):
        n = int(np.prod(shape)) * DT_SIZE[dtype]
        nn = (n + 31) // 32 * 32
        if pro.p + nn > pro.base + pro.size:
            pro.p = pro.base
        return M.sb(name, pro.take(n), dtype, shape)

    S.dma("sp", identf.ap, ident_d, writes=[identf])
    S.op("dve", lambda e: e.tensor_copy(identb.ap, identf.ap), reads=[identf], writes=[identb])
    S.op("pool", lambda e: e.memset(ones.ap, 1.0), writes=[ones])
    S.op("pool", lambda e: e.memset(negh.ap, -0.5), writes=[negh])
    S.op("pool", lambda e: e.memset(vpad.ap, 0.0), writes=[vpad])
    S.op("pool", lambda e: e.memset(ones_eo.ap, 0.0), writes=[ones_eo])
    S.op("pool", lambda e: e.memset(ones_eo.ap[:, 0, 0:64], 1.0), writes=[ones_eo])
    S.op("pool", lambda e: e.memset(ones_eo.ap[:, 1, 64:128], 1.0), writes=[ones_eo])
    for l in range(2):
        S.op("pool", lambda e, l=l: e.memset(halo[l].ap, 0.0), writes=[halo[l]])

    gst = ptmp("gst", F32, [128])
    gvecs = [W["norm_mix"][0], W["norm_ffn"][0], W["norm_ple"][0], W["norm_kv"], W["norm_mix"][1],
             W["norm_ffn"][1], W["norm_ple"][1]]
    G_MIX0, G_FFN0, G_PLE0, G_KV, G_MIX1, G_FFN1, G_PLE1 = range(7)
    for v, g in enumerate(gvecs):
        S.dma("sp", gst.ap[v * 8:(v + 1) * 8, :], g.rearrange("(k p) -> k p", p=128), writes=[gst])
    pt = ps_narrow("gT", F32, [56])
    S.op("pe", lambda e: e.transpose(pt.ap, gst.ap[0:56, :], identf.ap[0:56, 0:56]), reads=[gst, identf], writes=[pt])
    S.op("dve", lambda e: e.tensor_copy(gcols.ap, pt.ap), reads=[pt], writes=[gcols])

    def gcol(v, k):
        return gcols.ap[:, v * 8 + k:v * 8 + k + 1]

    for l in range(2):
        for pair, (va, vb) in enumerate([(W["f_conv_w"][l, 0], W["f_conv_w"][l, 1]),
                                         (W["f_conv_w"][l, 2], W["f_conv_b"][l])]):
            cst = ptmp("cst", F32, [128])
            S.dma("sp", cst.ap[0:44, :], va.rearrange("(m p) -> m p", p=128), writes=[cst])
            S.dma("sp", cst.ap[44:88, :], vb.rearrange("(m p) -> m p", p=128), writes=[cst])
            pc = ps_narrow("cT", F32, [88])
            S.op("pe", lambda e, pc=pc, cst=cst: e.transpose(pc.ap, cst.ap[0:88, :], identf.ap[0:88, 0:88]),
                 reads=[cst, identf], writes=[pc])
            S.op("dve", lambda e, pc=pc, l=l, pair=pair: e.tensor_copy(
                cw.ap[:, l, 2 * pair:2 * pair + 2, :], pc.ap.rearrange("p (a b) -> p a b", a=2)),
                reads=[pc], writes=[cw])

    def cwc(l, which, m):
        return cw.ap[:, l, which, m:m + 1]

    S.dma("sp", gv_tile.ap, W["a_norm_v"].broadcast_to([128, D]), writes=[gv_tile])
    S.dma("sp", gfin_tile.ap, W["norm_final"].rearrange("(o n) -> o n", o=1).broadcast_to([128, D]), writes=[gfin_tile])

    dist_i = ptmp("dist_i", I32, [128])
    dist = ptmp("dist", F32, [128])
    tri = ptmp("tri", F32, [128])
    ntri = ptmp("ntri", F32, [128])
    distc = ptmp("distc", F32, [128])
    dist128 = ptmp("dist128", F32, [128])
    S.op("pool", lambda e: e.iota(dist_i.ap, [[1, 128]], base=0, channel_multiplier=-1), writes=[dist_i])
    S.op("dve", lambda e: e.tensor_copy(dist.ap, dist_i.ap), reads=[dist_i], writes=[dist])
    S.op("dve", lambda e: e.tensor_scalar(tri.ap, dist.ap, 0.0, None, ALU.is_ge), reads=[dist], writes=[tri])
    S.op("dve", lambda e: e.tensor_scalar(ntri.ap, dist.ap, 0.0, None, ALU.is_lt), reads=[dist], writes=[ntri])
    S.op("dve", lambda e: e.tensor_scalar(distc.ap, dist.ap, 0.0, None, ALU.max), reads=[dist], writes=[distc])
    S.op("dve", lambda e: e.tensor_scalar(dist128.ap, dist.ap, 128.0, None, ALU.add), reads=[dist], writes=[dist128])

    wst = ptmp("wst", F32, [8, 128])
    S.dma("sp", wst.ap, W["a_w_s"][0].rearrange("h t s -> t h s"), writes=[wst])
    for hh in range(8):
        pw_ = ps_narrow("wsTp", F32, [128])
        S.op("pe", lambda e, pw_=pw_, hh=hh: e.transpose(pw_.ap, wst.ap[:, hh, :], identf.ap), reads=[wst, identf], writes=[pw_])
        S.op("dve", lambda e, pw_=pw_, hh=hh: e.tensor_tensor(wsT.ap[:, hh, :], pw_.ap, tri.ap, ALU.mult),
             reads=[pw_, tri], writes=[wsT])

    for g in range(4):
        for r in range(4):
            slope = float(2.0 ** (-8.0 * (4 * g + r + 1) / 16.0))
            tc_ = ptmp("mtc", F32, [128])
            tp_ = ptmp("mtp", F32, [128])
            S.op("act", lambda e, tc_=tc_, slope=slope: e.activation(tc_.ap, distc.ap, AF.Exp, scale=-slope), reads=[distc], writes=[tc_])
            S.op("dve", lambda e, tc_=tc_, g=g, r=r: e.tensor_tensor(Mk.ap[:, 1, g, r * 128:(r + 1) * 128], tc_.ap, tri.ap, ALU.mult),
                 reads=[tc_, tri], writes=[Mk])
            S.op("act", lambda e, tp_=tp_, slope=slope: e.activation(tp_.ap, dist128.ap, AF.Exp, scale=-slope), reads=[dist128], writes=[tp_])
            S.op("dve", lambda e, tp_=tp_, g=g, r=r: e.tensor_tensor(Mk.ap[:, 0, g, r * 128:(r + 1) * 128], tp_.ap, ntri.ap, ALU.mult),
                 reads=[tp_, ntri], writes=[Mk])

    def hilo(src_ap, n, hi_ap, lo_ap, src_tile, dst_tile):
        hf = ptmp("hf", F32, [n])
        lf = ptmp("lf", F32, [n])
        S.op("dve", lambda e: e.tensor_copy(hi_ap, src_ap), reads=[src_tile], writes=[dst_tile])
        S.op("dve", lambda e: e.tensor_copy(hf.ap[0:1, :], hi_ap), reads=[dst_tile], writes=[hf])
        S.op("dve", lambda e: e.tensor_tensor(lf.ap[0:1, :], src_ap, hf.ap[0:1, :], ALU.subtract), reads=[src_tile, hf], writes=[lf])
        S.op("dve", lambda e: e.tensor_copy(lo_ap, lf.ap[0:1, :]), reads=[lf], writes=[dst_tile])

    for half in range(2):
        src = W["b_sinks"].rearrange("o (jp gi r) -> o jp gi r", jp=2, gi=2, r=4)[:, :, half, :]
        S.dma("sp", sinkcol.ap[half * 64:(half + 1) * 64, :, :], src.broadcast_to([64, 2, 4]), writes=[sinkcol])
    S.op("act", lambda e: e.activation(sinkcol.ap, sinkcol.ap, AF.Exp), reads=[sinkcol], writes=[sinkcol])
    S.op("dve", lambda e: e.tensor_copy(sinkt.ap.rearrange("p j (r i) -> p j r i", r=4),
                                        sinkcol.ap.unsqueeze(3).broadcast_to([128, 2, 4, 128])), reads=[sinkcol], writes=[sinkt])
    bsr = ptmp("bsr", F32, [D])
    S.dma("sp", bsr.ap[0:1, :], W["a_b_s"].rearrange("o h t -> o (h t)"), writes=[bsr])
    hilo(bsr.ap[0:1, :], D, bs_hl.ap[0:1, 0, :], bs_hl.ap[0:1, 1, :], bsr, bs_hl)
    for l in range(2):
        bgr = ptmp("bgr", F32, [D])
        S.dma("sp", bgr.ap[0:1, :], W["ple_b_gate"][l:l + 1, :], writes=[bgr])
        hilo(bgr.ap[0:1, :], D, bg_hl.ap[0:1, l, 0, :], bg_hl.ap[0:1, l, 1, :], bgr, bg_hl)

    rot = {"i": 0}

    def cast(dst_ap, src_ap, scale_ap, reads, writes):
        i = rot["i"]
        rot["i"] += 1
        if i % 2 == 0:
            sc = scale_ap if scale_ap is not None else 1.0
            S.op("act", lambda e: e.activation(dst_ap, src_ap, AF.Copy, scale=sc), reads=reads, writes=writes)
        else:
            sc = scale_ap if scale_ap is not None else 1.0
            S.op("dve", lambda e: e.tensor_scalar(dst_ap, src_ap, sc, None, ALU.mult), reads=reads, writes=writes)

    def prep_plain(Wap, KD, N, gain, dst, dst_fn, cb=None, rows_fn=None, perm_q=False):
        cb = cb or N
        for kb in range(KD // 128):
            for ci in range(N // cb):
                st = ptmp("wst_", F32, [cb])
                if rows_fn is None:
                    S.dma("sp", st.ap, Wap[kb * 128:(kb + 1) * 128, ci * cb:(ci + 1) * cb], writes=[st])
                else:
                    for (p0, r0, nr) in rows_fn(kb):
                        S.dma("sp", st.ap[p0:p0 + nr, :], Wap[r0:r0 + nr, ci * cb:(ci + 1) * cb], writes=[st])
                so = ptmp("wso_", BF16, [cb])
                sc = gcol(gain, kb) if gain is not None else None
                if not perm_q:
                    cast(so.ap, st.ap, sc, [st, gcols], [so])
                else:
                    for half in range(2):
                        o_ap = so.ap.rearrange("p (c x) -> p c x", x=128)[:, :, half * 64:(half + 1) * 64].rearrange("p (a r) d -> p a r d", a=2)
                        i_ap = st.ap.rearrange("p (a hh d) -> p a hh d", a=2, hh=8)[:, :, 4 * half:4 * half + 4, :]
                        S.op("act", lambda e, o_ap=o_ap, i_ap=i_ap, sc=sc: e.activation(o_ap, i_ap, AF.Copy, scale=sc),
                             reads=[st, gcols], writes=[so])
                for (d_ap, s_ap, part) in dst_fn(kb, ci, so):
                    S.dma("pool", d_ap, s_ap, reads=[so], writes=[(dst, part)])

    a_w_in = W["a_w_in"][0]
    for kb in range(KC):
        st = ptmp("wst_", F32, [2 * D])
        S.dma("sp", st.ap, a_w_in[kb * 128:(kb + 1) * 128, :], writes=[st])
        so = ptmp("wso_", BF16, [2 * D])
        cast(so.ap, st.ap, gcol(G_MIX0, kb), [st, gcols], [so])
        S.dma("pool", s_win_u.ap[:, :, kb, :], so.ap[:, 0:D].rearrange("p (j x) -> p j x", j=4), reads=[so], writes=[(s_win_u, kb)])
        S.dma("pool", s_win_v.ap[:, kb, :], so.ap[:, D:2 * D], reads=[so], writes=[(s_win_v, kb)])
    prep_plain(W["a_w_out"][0], D, D, None, s_wout, lambda kb, ci, so: [(s_wout.ap[:, kb, :], so.ap, kb)])
    for l in range(2):
        prep_plain(W["f_w_up"][l], D, 2 * DFF, G_FFN0 if l == 0 else G_FFN1, s_wup[l],
                   lambda kb, ci, so, l=l: [(s_wup[l].ap[:, :, kb, ci * 256:(ci + 1) * 256],
                                             so.ap.rearrange("p (i x) -> p i x", i=11), kb * 2 + ci)], cb=DFF)
        prep_plain(W["f_w_down"][l], DFF, D, None, s_wdown[l], lambda kb, ci, so, l=l: [(s_wdown[l].ap[:, kb, :], so.ap, kb)])
        prep_plain(W["ple_w_gate"][l], D, D, G_PLE0 if l == 0 else G_PLE1, s_wgate[l],
                   lambda kb, ci, so, l=l: [(s_wgate[l].ap[:, kb, :], so.ap, kb)])
        prep_plain(W["ple_w_in"][l], PD, D, None, s_plein[l], lambda kb, ci, so, l=l: [(s_plein[l].ap[:, kb, :], so.ap, kb)])
    prep_plain(W["w_kv"], D, 512, G_KV, s_wkv, lambda kb, ci, so: [(s_wkv.ap[:, kb, :], so.ap, kb)])
    prep_plain(W["b_w_q"][0], D, D, G_MIX1, s_wq,
               lambda kb, ci, so: [(s_wq.ap[:, :, kb, :], so.ap.rearrange("p (j x) -> p j x", j=4), kb)], perm_q=True)
    prep_plain(W["b_w_o"][0], D, D, None, s_wo, lambda kb, ci, so: [(s_wo.ap[:, kb, :], so.ap, kb)],
               rows_fn=lambda c: [(0, hL(c) * 64, 64), (64, hU(c) * 64, 64)])

    WD_K = [0, 6, 11, 17, 22]
    WD_AFTER = {1: 0, 3: 1, 5: 2, 7: 3}
    sched = []
    for ti in range(NT):
        for j in range(4):
            sched.append((("win_u", ti, j), s_win_u, s_win_u.ap[:, j], [8, 256]))
        sched.append((("win_v", ti), s_win_v, s_win_v.ap, [8, 1024]))
        sched.append((("wout", ti), s_wout, s_wout.ap, [8, 1024]))
        for l in range(2):
            if l == 1:
                sched.append((("wkv", ti), s_wkv, s_wkv.ap, [8, 512]))
                for j in range(4):
                    sched.append((("wq", ti, j), s_wq, s_wq.ap[:, j], [8, 256]))
                sched.append((("wo", ti), s_wo, s_wo.ap, [8, 1024]))
            for i in range(11):
                sched.append((("wup", ti, l, i), s_wup[l], s_wup[l].ap[:, i], [8, 512]))
            for q in range(4):
                k0, k1 = WD_K[q], WD_K[q + 1]
                sched.append((("wdown", ti, l, q), s_wdown[l], s_wdown[l].ap[:, k0:k1], [k1 - k0, 1024]))
            sched.append((("wgate", ti, l), s_wgate[l], s_wgate[l].ap, [8, 1024]))
            sched.append((("plein", ti, l), s_plein[l], s_plein[l].ap, [2, 1024]))
    ring = Ring(S, M, ring_base, ring_size, sched)

    def mm_group(specs, reads, writes):
        fns = [(lambda e, o=o, l_=l_, r_=r_, st=st, sp=sp: e.matmul(o, lhsT=l_, rhs=r_, start=st, stop=sp))
               for (o, l_, r_, st, sp) in specs]
        S.op_multi("pe", fns, reads=reads, writes=writes)

    xtok_i = {"i": 0}

    def make_phases(h):
        def rstd_chain(src_ap, src_acc, st):
            S.op("act", lambda e: e.activation(junk.ap, src_ap, AF.Square, accum_out=st.ap[:, 0:1]), reads=[src_acc], writes=[junk, st])
            S.op("dve", lambda e: e.tensor_scalar(st.ap[:, 1:2], st.ap[:, 0:1], 1.0 / D, EPS, ALU.mult, ALU.add), reads=[st], writes=[st])
            S.op("pool", lambda e: e.tensor_tensor(st.ap[:, 2:3], st.ap[:, 1:2], negh.ap, ALU.pow), reads=[st, negh], writes=[st])

        def norm_a(c, st):
            rstd_chain(h.ap[:, c, :], (h, c), st)
            xt = xn_tok[xtok_i["i"] % len(xn_tok)]
            xtok_i["i"] += 1
            S.op("act", lambda e: e.activation(xt.ap, h.ap[:, c, :], AF.Copy, scale=st.ap[:, 2:3]), reads=[(h, c), st], writes=[xt])
            return xt

        def norm_b(c, xn, xt, tp_alloc=None):
            tp = (tp_alloc or ps_narrow)("tp", BF16, [KC, 128])
            fns = [(lambda e, k=k: e.transpose(tp.ap[:, k, :], xt.ap[:, k * 128:(k + 1) * 128], identb.ap)) for k in range(KC)]
            S.op_multi("pe", fns, reads=[xt, identb], writes=[tp])
            S.op("dve", lambda e: e.tensor_copy(xn.ap[:, :, c * 128:(c + 1) * 128], tp.ap), reads=[tp], writes=[(xn, c)])

        def norm_chunk(c, xn, st):
            norm_b(c, xn, norm_a(c, st))

        class Deferred:
            def __init__(self):
                self.q = []

            def push(self, fn):
                self.q.append(fn)

            def flush(self, keep=0):
                while len(self.q) > keep:
                    self.q.pop(0)()

        def final_chunk(ti, c, st, ot):
            rstd_chain(h.ap[:, c, :], (h, c), st)
            S.op("dve", lambda e: e.scalar_tensor_tensor(ot.ap, h.ap[:, c, :], st.ap[:, 2:3], gfin_tile.ap, ALU.mult, ALU.mult),
                 reads=[(h, c), st, gfin_tile], writes=[ot])
            r0 = ti * T + c * 128
            S.dma("pool", out_d[r0:r0 + 128, :], ot.ap, reads=[ot], writes=[(out_t, r0 // 128)])

        def load_p(ti):
            for l in range(2):
                ph = Phase()
                pst = ph.t("pst", F32, [NCH, PD])
                pb = ph.t("pb", BF16, [NCH, PD])
                S.dma("act", pst.ap, p_d[l, ti * T:(ti + 1) * T, :].rearrange("(c p) d -> p c d", p=128), writes=[pst])
                S.op("pool", lambda e, pb=pb, pst=pst: e.tensor_copy(pb.ap, pst.ap), reads=[pst], writes=[pb])
                ptp = ps_narrow("ptp", BF16, [2, T])
                fns = [(lambda e, c=c, j=j, ptp=ptp, pb=pb: e.transpose(ptp.ap[:, j, c * 128:(c + 1) * 128], pb.ap[:, c, j * 128:(j + 1) * 128], identb.ap))
                       for c in range(NCH) for j in range(2)]
                S.op_multi("pe", fns, reads=[pb, identb], writes=[ptp])
                S.op("dve", lambda e, ptp=ptp, l=l: e.tensor_copy(pT[l].ap, ptp.ap), reads=[ptp], writes=[pT[l]])

        def mixer_a(ti, xn, xn_next):
            ph = Phase()
            uT = ph.t("uT", F32, [KC, T], nparts=KC)
            usT = ph.t("usT", BF16, [KC, T], nparts=NCH)
            gvs = [ph.t(f"gv{i}", F32, [D]) for i in range(2)]
            vns = [ph.t(f"vn{i}", BF16, [D]) for i in range(2)]
            for j in range(4):
                wu = ring.get(("win_u", ti, j))
                for cc in range(2):
                    m = 2 * j + cc
                    pu = ps_narrow("pu", F32, [T])
                    mm_group([(pu.ap, wu.ap[:, k, cc * 128:(cc + 1) * 128], xn.ap[:, k, :], k == 0, k == KC - 1) for k in range(KC)],
                             reads=[wu, xn], writes=[pu])
                    S.op("act", lambda e, pu=pu, m=m: e.activation(uT.ap[:, m, :], pu.ap, AF.Gelu_apprx_tanh), reads=[pu], writes=[(uT, m)])
                ring.release(("win_u", ti, j))
            wv = ring.get(("win_v", ti))
            wo = ring.get(("wout", ti))
            state = {}

            def emit_v(c):
                pv = ps_wide("pv", F32, [D])
                mm_group([(pv.ap[:, hf * 512:(hf + 1) * 512], xn.ap[:, k, c * 128:(c + 1) * 128], wv.ap[:, k, hf * 512:(hf + 1) * 512], k == 0, k == KC - 1)
                          for hf in range(2) for k in range(KC)], reads=[(xn, c), wv], writes=[pv])
                gv = gvs[c % 2]
                vn = vns[c % 2]
                st = vstat[c]
                S.op("act", lambda e: e.activation(gv.ap, pv.ap, AF.Gelu_apprx_tanh), reads=[pv], writes=[gv])
                rstd_chain(gv.ap, gv, st)
                S.op("dve", lambda e: e.scalar_tensor_tensor(vn.ap, gv.ap, st.ap[:, 2:3], gv_tile.ap, ALU.mult, ALU.mult),
                     reads=[gv, st, gv_tile], writes=[vn])
                state[c] = vn

            def emit_s(c):
                vn = state[c]
                ps_ = ps_wide("ps", F32, [8, 128])
                specs = []
                for hh in range(8):
                    specs.append((ps_.ap[:, hh, :], vn.ap[:, hh * 128:(hh + 1) * 128], wsT.ap[:, hh, :], True, False))
                    specs.append((ps_.ap[:, hh, :], ones.ap[0:1, 0:128], bs_hl.ap[0:1, 0, hh * 128:(hh + 1) * 128], False, False))
                    specs.append((ps_.ap[:, hh, :], ones.ap[0:1, 0:128], bs_hl.ap[0:1, 1, hh * 128:(hh + 1) * 128], False, True))
                mm_group(specs, reads=[vn, wsT, ones, bs_hl], writes=[ps_])
                S.op("dve", lambda e: e.tensor_tensor(usT.ap[:, :, c * 128:(c + 1) * 128], ps_.ap, uT.ap[:, :, c * 128:(c + 1) * 128], ALU.mult),
                     reads=[ps_, uT], writes=[(usT, c)])

            def emit_o(c):
                po = ps_wide("po", F32, [D])
                mm_group([(po.ap[:, hf * 512:(hf + 1) * 512], usT.ap[:, k, c * 128:(c + 1) * 128], wo.ap[:, k, hf * 512:(hf + 1) * 512], k == 0, k == KC - 1)
                          for hf in range(2) for k in range(KC)], reads=[(usT, c), wo], writes=[po])
                S.op("dve", lambda e: e.tensor_tensor(h.ap[:, c, :], po.ap, h.ap[:, c, :], ALU.add), reads=[po, (h, c)], writes=[(h, c)])
                xt = norm_a(c, stats[1][c])
                dfr.push(lambda: norm_b(c, xn_next, xt))

            dfr = Deferred()
            for i in range(NCH + 2):
                if i < NCH:
                    emit_v(i)
                if 1 <= i <= NCH:
                    emit_s(i - 1)
                if i >= 2:
                    emit_o(i - 2)
                    dfr.flush(keep=1)
            dfr.flush()
            ring.release(("win_v", ti))
            ring.release(("wout", ti))

        def ffn(ti, l, xn, xn_next, st_idx):
            ph = Phase()
            actT = ph.t("actT", BF16, [NMH, T], nparts=NMH)
            atl = [ph.t(f"a{i}", F32, [T]) for i in range(8)]
            ai = {"i": 0}
            hl = halo[l]
            S.op("dve", lambda e: e.tensor_tensor(ctmp.ap[:, 0, :], hl.ap[:, :, 1], cw.ap[:, l, 1, :], ALU.mult), reads=[hl, cw], writes=[ctmp])
            S.op("dve", lambda e: e.tensor_tensor(ctmp.ap[:, 1, :], hl.ap[:, :, 0], cw.ap[:, l, 0, :], ALU.mult), reads=[hl, cw], writes=[ctmp])
            S.op("dve", lambda e: e.tensor_tensor(corr.ap[:, :, 0], ctmp.ap[:, 0, :], ctmp.ap[:, 1, :], ALU.add), reads=[ctmp], writes=[corr])
            S.op("dve", lambda e: e.tensor_tensor(corr.ap[:, :, 1], hl.ap[:, :, 1], cw.ap[:, l, 0, :], ALU.mult), reads=[hl, cw], writes=[corr])
            def stage1(w, mp, cc):
                res = []
                phs = []
                for part in range(2):
                    col0 = part * 256 + cc * 128
                    pH = ps_narrow("pH", F32, [T])
                    mm_group([(pH.ap, w.ap[:, k, col0:col0 + 128], xn.ap[:, k, :], k == 0, k == KC - 1) for k in range(KC)],
                             reads=[w, xn], writes=[pH])
                    phs.append(pH)
                for part in range(2):
                    m = mp + NMH * part
                    pH = phs[part]
                    a = atl[ai["i"] % len(atl)]
                    ai["i"] += 1
                    S.op("act", lambda e, a=a, pH=pH, m=m: e.activation(a.ap, pH.ap, AF.Identity, scale=cwc(l, 2, m), bias=cwc(l, 3, m)),
                         reads=[pH, cw], writes=[a])
                    res.append(a)
                    S.op("act", lambda e, pH=pH, m=m: e.activation(hl.ap[:, m, :], pH.ap[:, T - 2:T], AF.Copy), reads=[pH], writes=[hl])
                for part in range(2):
                    m = mp + NMH * part
                    pH = phs[part]
                    a = res[part]
                    S.op("dve", lambda e, a=a, pH=pH, m=m: e.scalar_tensor_tensor(a.ap[:, 1:T], pH.ap[:, 0:T - 1], cwc(l, 1, m), a.ap[:, 1:T], ALU.mult, ALU.add),
                         reads=[pH, cw, a], writes=[a])
                    S.op("dve", lambda e, a=a, pH=pH, m=m: e.scalar_tensor_tensor(a.ap[:, 2:T], pH.ap[:, 0:T - 2], cwc(l, 0, m), a.ap[:, 2:T], ALU.mult, ALU.add),
                         reads=[pH, cw, a], writes=[a])
                    S.op("pool", lambda e, a=a, m=m: e.tensor_tensor(a.ap[:, 0:2], a.ap[:, 0:2], corr.ap[:, m, :], ALU.add),
                         reads=[a, corr], writes=[a])
                return res

            def stage2(mp, res):
                ag, au = res
                S.op("act", lambda e: e.activation(ag.ap, ag.ap, AF.Silu), reads=[ag], writes=[ag])
                S.op("pool", lambda e: e.tensor_tensor(actT.ap[:, mp, :], ag.ap, au.ap, ALU.mult), reads=[ag, au], writes=[(actT, mp)])

            pend = []
            for i in range(11):
                w = ring.get(("wup", ti, l, i))
                for cc in range(2):
                    mp = 2 * i + cc
                    res = stage1(w, mp, cc)
                    pend.append((mp, res, i if cc == 1 else None))
                    if len(pend) > 1:
                        pmp, pres, prel = pend.pop(0)
                        stage2(pmp, pres)
                        if prel is not None:
                            ring.release(("wup", ti, l, prel))
            while pend:
                pmp, pres, prel = pend.pop(0)
                stage2(pmp, pres)
                if prel is not None:
                    ring.release(("wup", ti, l, prel))
            wds = [ring.get(("wdown", ti, l, q)) for q in range(4)]

            def wd_of(mp):
                q = max(i for i in range(4) if WD_K[i] <= mp)
                return wds[q], mp - WD_K[q]
            dfr = Deferred()
            for c in range(NCH):
                po = ps_wide("pd", F32, [D])
                specs = []
                for hf in range(2):
                    for mp in range(NMH):
                        w_, kk = wd_of(mp)
                        specs.append((po.ap[:, hf * 512:(hf + 1) * 512], actT.ap[:, mp, c * 128:(c + 1) * 128],
                                      w_.ap[:, kk, hf * 512:(hf + 1) * 512], mp == 0, mp == NMH - 1))
                mm_group(specs, reads=[actT] + wds, writes=[po])
                S.op("dve", lambda e, po=po, c=c: e.tensor_tensor(h.ap[:, c, :], po.ap, h.ap[:, c, :], ALU.add), reads=[po, (h, c)], writes=[(h, c)])
                dfr.flush()
                xt = norm_a(c, stats[st_idx][c])
                dfr.push(lambda c=c, xt=xt: norm_b(c, xn_next, xt))
            dfr.flush()
            for q in range(4):
                ring.release(("wdown", ti, l, q))

        def ple(ti, l, xn, after_chunk):
            ph = Phase()
            gates = [ph.t(f"gate{i}", F32, [D]) for i in range(2)]
            tmps = [ph.t(f"ptmp{i}", F32, [D]) for i in range(2)]
            wg = ring.get(("wgate", ti, l))
            wp = ring.get(("plein", ti, l))
            dfr = Deferred()
            for c in range(NCH):
                pg = ps_wide("pg", F32, [D])
                specs = []
                for hf in range(2):
                    o = pg.ap[:, hf * 512:(hf + 1) * 512]
                    for k in range(KC):
                        specs.append((o, xn.ap[:, k, c * 128:(c + 1) * 128], wg.ap[:, k, hf * 512:(hf + 1) * 512], k == 0, False))
                    specs.append((o, ones.ap[0:1, 0:128], bg_hl.ap[0:1, l, 0, hf * 512:(hf + 1) * 512], False, False))
                    specs.append((o, ones.ap[0:1, 0:128], bg_hl.ap[0:1, l, 1, hf * 512:(hf + 1) * 512], False, True))
                mm_group(specs, reads=[(xn, c), wg, ones, bg_hl], writes=[pg])
                gate = gates[c % 2]
                tmp = tmps[c % 2]
                S.op("act", lambda e, gate=gate, pg=pg: e.activation(gate.ap, pg.ap, AF.Sigmoid), reads=[pg], writes=[gate])
                pw = ps_wide("pw", F32, [D])
                mm_group([(pw.ap[:, hf * 512:(hf + 1) * 512], pT[l].ap[:, j, c * 128:(c + 1) * 128], wp.ap[:, j, hf * 512:(hf + 1) * 512], j == 0, j == 1)
                          for hf in range(2) for j in range(2)], reads=[pT[l], wp], writes=[pw])
                S.op("dve", lambda e, tmp=tmp, pw=pw, gate=gate: e.tensor_tensor(tmp.ap, pw.ap, gate.ap, ALU.mult), reads=[pw, gate], writes=[tmp])
                S.op("pool", lambda e, tmp=tmp, c=c: e.tensor_tensor(h.ap[:, c, :], h.ap[:, c, :], tmp.ap, ALU.add), reads=[(h, c), tmp], writes=[(h, c)])
                dfr.flush()
                r = after_chunk(c)
                if r is not None:
                    dfr.push(r)
            dfr.flush()
            ring.release(("wgate", ti, l))
            ring.release(("plein", ti, l))

        def attention(ti, xn, xn_next):
            ph = Phase()
            qT = ph.t("qT", BF16, [KC, T], nparts=KC)
            oT = ph.t("oT", BF16, [KC, T], nparts=NCH)
            NPT = 8
            pts = [ph.t(f"PT{i}", BF16, [512]) for i in range(NPT)]
            lnDs = [ph.t(f"lnD{i}", F32, [512]) for i in range(2)]
            Rs = [ph.t(f"R{i}", F32, [512]) for i in range(2)]
            SB = [0, 1, 2, 3]

            def ps_S(name, dtype, shape):
                return ps_from("S", SB, name, dtype, shape)

            wkv = ring.get(("wkv", ti))
            for j in range(2):
                pk = ps_narrow("pk", F32, [T])
                mm_group([(pk.ap, wkv.ap[:, k, j * 128:(j + 1) * 128], xn.ap[:, k, :], k == 0, k == KC - 1) for k in range(KC)],
                         reads=[wkv, xn], writes=[pk])
                S.op("act", lambda e, pk=pk, j=j: e.activation(kT.ap[:, j, 128:128 + T], pk.ap, AF.Copy), reads=[pk],
                     writes=[(kT, list(range(1, NCH + 1)))])
            for c in range(NCH):
                pv = ps_narrow("pvt", F32, [256])
                mm_group([(pv.ap, xn.ap[:, k, c * 128:(c + 1) * 128], wkv.ap[:, k, 256:512], k == 0, k == KC - 1) for k in range(KC)],
                         reads=[(xn, c), wkv], writes=[pv])
                pv4 = pv.ap.rearrange("p (j gi d) -> p j gi d", gi=2, d=64)
                vp4 = vpad.ap[:, 1 + c, :, :].rearrange("p (j gi) x -> p j gi x", gi=2)
                S.op("dve", lambda e, pv4=pv4, vp4=vp4: e.tensor_copy(vp4[:, :, 0, 0:64], pv4[:, :, 0, :]), reads=[pv], writes=[(vpad, 1 + c)])
                S.op("dve", lambda e, pv4=pv4, vp4=vp4: e.tensor_copy(vp4[:, :, 1, 64:128], pv4[:, :, 1, :]), reads=[pv], writes=[(vpad, 1 + c)])
            ring.release(("wkv", ti))
            for j in range(4):
                wq = ring.get(("wq", ti, j))
                for cc in range(2):
                    m = 2 * j + cc
                    pq = ps_narrow("pq", F32, [T])
                    mm_group([(pq.ap, wq.ap[:, k, cc * 128:(cc + 1) * 128], xn.ap[:, k, :], k == 0, k == KC - 1) for k in range(KC)],
                             reads=[wq, xn], writes=[pq])
                    S.op("act", lambda e, pq=pq, m=m: e.activation(qT.ap[:, m, :], pq.ap, AF.Copy), reads=[pq], writes=[(qT, m)])
                ring.release(("wq", ti, j))
            wo = ring.get(("wo", ti))
            cnt = {"e": 0, "r": 0}

            def unit_scores(c, jp, gi):
                g = 2 * jp + gi
                pb_ = gi * 64
                nb = ti * NCH + c
                whichs = [0, 1] if nb > 0 else [1]
                outl = []
                for which in whichs:
                    blk = c + which
                    pS = ps_S("pS", F32, [512])
                    mm_group([(pS.ap, kT.ap[pb_:pb_ + 64, jp, blk * 128:(blk + 1) * 128],
                               qT.ap[pb_:pb_ + 64, jp * 4:(jp + 1) * 4, c * 128:(c + 1) * 128], True, True)],
                             reads=[(kT, blk), qT], writes=[pS])
                    pt_ = pts[cnt["e"] % NPT]
                    cnt["e"] += 1
                    S.op("act", lambda e, pt_=pt_, pS=pS: e.activation(pt_.ap, pS.ap, AF.Exp, scale=0.125), reads=[pS], writes=[pt_])
                    S.op("dve" if which == 1 else "pool",
                         lambda e, pt_=pt_, which=which, g=g: e.tensor_tensor(pt_.ap, pt_.ap, Mk.ap[:, which, g, :], ALU.mult),
                         reads=[pt_, Mk], writes=[pt_])
                    outl.append((pt_, blk, which))
                return outl

            def unit_pv(c, jp, gi, pl, O, Dn):
                g = 2 * jp + gi
                specs = []
                reads = [ones_eo, vpad]
                first = gi == 0
                for wi, (pt_, blk, which) in enumerate(pl):
                    st_ = first and wi == 0
                    specs.append((O.ap, vpad.ap[:, blk, g, :], pt_.ap, st_, gi == 1 and wi == len(pl) - 1))
                    specs.append((Dn.ap, ones_eo.ap[:, gi, :], pt_.ap, st_, False))
                    reads += [pt_]
                if gi == 1:
                    o_, l_, r_, st_, _ = specs[-1]
                    specs[-1] = (o_, l_, r_, st_, True)
                mm_group(specs, reads=reads, writes=[O, Dn])

            def finish_pair(c, jp, O, Dn):
                lnD = lnDs[cnt["r"] % 2]
                R = Rs[cnt["r"] % 2]
                cnt["r"] += 1
                S.op("dve", lambda e: e.tensor_tensor(lnD.ap, Dn.ap, sinkt.ap[:, jp, :], ALU.add), reads=[Dn, sinkt], writes=[lnD])
                S.op("act", lambda e: e.activation(lnD.ap, lnD.ap, AF.Ln), reads=[lnD], writes=[lnD])
                S.op("act", lambda e: e.activation(R.ap, lnD.ap, AF.Exp, scale=-1.0), reads=[lnD], writes=[R])
                S.op("dve", lambda e: e.tensor_tensor(oT.ap[:, jp * 4:(jp + 1) * 4, c * 128:(c + 1) * 128],
                                                      O.ap.rearrange("p (r i) -> p r i", r=4), R.ap.rearrange("p (r i) -> p r i", r=4), ALU.mult),
                     reads=[O, R], writes=[(oT, c)])

            dfr = Deferred()

            def emit_wo(c):
                psr["n"] += 1
                po = M.ps(f"pwo_{psr['n']}", 0, F32, [D], nbanks=2)
                mm_group([(po.ap[:, hf * 512:(hf + 1) * 512], oT.ap[:, k, c * 128:(c + 1) * 128], wo.ap[:, k, hf * 512:(hf + 1) * 512], k == 0, k == KC - 1)
                          for hf in range(2) for k in range(KC)], reads=[(oT, c), wo], writes=[po])
                S.op("dve", lambda e: e.tensor_tensor(h.ap[:, c, :], po.ap, h.ap[:, c, :], ALU.add), reads=[po, (h, c)], writes=[(h, c)])
                xt = norm_a(c, stats[4][c])
                dfr.push(lambda: norm_b(c, xn_next, xt, tp_alloc=ps_S))

            units = [(c, jp, gi) for c in range(NCH) for jp in range(2) for gi in range(2)]
            SK = 1
            N = len(units)
            pls = {}
            wo_q = []
            fin_q = []
            OD = {}
            for idx in range(N + SK + 3):
                if idx < N:
                    pls[idx] = unit_scores(*units[idx])
                dfr.flush()
                for cq in wo_q:
                    emit_wo(cq)
                wo_q = []
                for (c_, jp_, O_, Dn_) in fin_q:
                    finish_pair(c_, jp_, O_, Dn_)
                    if jp_ == 1:
                        wo_q.append(c_)
                fin_q = []
                j = idx - SK
                if 0 <= j < N:
                    c, jp, gi = units[j]
                    if gi == 0:
                        psr["n"] += 1
                        b0 = 4 + 2 * ((2 * c + jp) % 2)
                        OD[(c, jp)] = (M.ps(f"O_{psr['n']}", b0, F32, [512]), M.ps(f"Dn_{psr['n']}", b0 + 1, F32, [512]))
                    O, Dn = OD[(c, jp)]
                    unit_pv(c, jp, gi, pls.pop(j), O, Dn)
                    if gi == 1:
                        fin_q.append((c, jp, O, Dn))
            assert not wo_q and not fin_q
            dfr.flush()
            ring.release(("wo", ti))
            S.op("pool", lambda e: e.tensor_copy(kT.ap[:, :, 0:128], kT.ap[:, :, T:T + 128]), reads=[(kT, NCH)], writes=[(kT, 0)])
            S.op("pool", lambda e: e.tensor_copy(vpad.ap[:, 0, :, :], vpad.ap[:, NCH, :, :]), reads=[(vpad, NCH)], writes=[(vpad, 0)])
        return dict(norm_chunk=norm_chunk, norm_a=norm_a, norm_b=norm_b, final_chunk=final_chunk, load_p=load_p,
                    mixer_a=mixer_a, ffn=ffn, ple=ple, attention=attention)

    PH = [make_phases(hA), make_phases(hB)]
    for c in range(NCH):
        S.dma("sp", hA.ap[:, c, :], x_d[c * 128:(c + 1) * 128, :], writes=[(hA, c)])
    ring.prefetch()
    for c in range(NCH):
        PH[0]["norm_chunk"](c, xn_a, stats[0][c])
    for ti in range(NT):
        P = PH[ti % 2]
        Pn = PH[(ti + 1) % 2]
        hn = hs[(ti + 1) % 2]
        if ti + 1 < NT:
            for c in range(NCH):
                r1 = (ti + 1) * T + c * 128
                S.dma("sp", hn.ap[:, c, :], x_d[r1:r1 + 128, :], writes=[(hn, c)])
        P["load_p"](ti)
        P["mixer_a"](ti, xn_a, xn_b)
        P["ffn"](ti, 0, xn_b, xn_a, 2)

        def after0(c, P=P):
            xt = P["norm_a"](c, stats[3][c])
            return lambda: P["norm_b"](c, xn_b, xt)
        P["ple"](ti, 0, xn_a, after0)
        P["attention"](ti, xn_b, xn_a)
        P["ffn"](ti, 1, xn_a, xn_b, 5)
        if ti + 1 < NT:
            for c in range(NCH):
                Pn["norm_chunk"](c, xn_a, stats[0][c])
        phf = Phase()
        phf.b.take(16 * 1024)
        ots = [phf.t(f"ot{i}", F32, [D]) for i in range(2)]
        P["ple"](ti, 1, xn_b, lambda c, ti=ti, P=P, ots=ots: P["final_chunk"](ti, c, stats[6][c], ots[c % 2]))
    S.final_wait("pool", [out_t])
    S.emit()
    return nc, S


_CACHE = {}


def kernel(**inputs):
    x = np.asarray(inputs["x"], dtype=np.float32)
    p = np.asarray(inputs["p"], dtype=np.float32)
    B, S_len, _ = x.shape
    key = (S_len,)
    if key not in _CACHE:
        _CACHE[key] = build_program(S_len)[0]
    nc = _CACHE[key]
    ident = np.eye(128, dtype=np.float32)
    shared = {k: np.ascontiguousarray(np.asarray(inputs[k], dtype=np.float32)) for k in INPUT_SHAPES}
    in_maps = []
    for b in range(B):
        m = dict(shared)
        m["x"] = np.ascontiguousarray(x[b])
        m["p"] = np.ascontiguousarray(p[:, b])
        m["ident"] = ident
        in_maps.append(m)
    res = run_bass_kernel_spmd(nc, in_maps, core_ids=list(range(B)))
    return np.stack([np.asarray(r["out"], dtype=np.float32) for r in res.results], axis=0)
```

```python
from concourse.bass_utils import run_bass_kernel_spmd

import numpy as np
import concourse.bass as bass
import concourse.mybir as mybir

F32 = mybir.dt.float32
BF16 = mybir.dt.bfloat16
I32 = mybir.dt.int32
AF = mybir.ActivationFunctionType
ALU = mybir.AluOpType

DT_SIZE = {F32: 4, BF16: 2, I32: 4}
STRICT_SAME_ENGINE = True


class Tile:
    def __init__(self, name, space, off, nbytes, ap, nparts=1):
        self.name = name
        self.space = space
        self.off = off
        self.nbytes = nbytes
        self.ap = ap
        self.nparts = nparts
        self.w = [dict() for _ in range(nparts)]
        self.r = [dict() for _ in range(nparts)]

    def __getitem__(self, key):
        return self.ap[key]

    def all_events(self):
        ev = {}
        for d in self.w + self.r:
            for k, v in d.items():
                if ev.get(k, 0) < v:
                    ev[k] = v
        return ev

    def inherit(self, ev):
        for p in range(self.nparts):
            for k, v in ev.items():
                if self.r[p].get(k, 0) < v:
                    self.r[p][k] = v


class Acc:
    def __init__(self, tile, parts=None):
        self.tile = tile
        self.parts = range(tile.nparts) if parts is None else parts


def _acc(x):
    if isinstance(x, Acc):
        return x
    if isinstance(x, Tile):
        return Acc(x)
    t, p = x
    if isinstance(p, int):
        p = [p]
    return Acc(t, p)


class Sched:
    COMPUTE = ("pe", "act", "dve", "pool")
    NDMA_SEMS = 12

    def __init__(self, nc, dma_queues=("sp", "act")):
        self.nc = nc
        self.h = {"pe": nc.tensor, "act": nc.scalar, "dve": nc.vector, "pool": nc.gpsimd, "sp": nc.sync}
        self.streams = {e: [] for e in self.h}
        self.sems = {}
        self.cnt = {}
        self.seen = {e: {} for e in self.h}
        for e in self.COMPUTE:
            self.sems[e] = nc.alloc_semaphore(name=f"s_{e}")
            self.cnt[e] = 0
        self.dma_rr = {}
        for q in dma_queues:
            self.dma_rr[q] = 0
            for i in range(self.NDMA_SEMS):
                k = f"d_{q}{i}"
                self.sems[k] = nc.alloc_semaphore(name=k)
                self.cnt[k] = 0
        self.n_ops = 0
        self.n_waits = 0

    def _deps(self, eng, reads, writes, strict=False):
        deps = {}

        def add(k, v, raw):
            if k == eng and not strict:
                if eng == "pe" or not (raw or STRICT_SAME_ENGINE):
                    return
            if deps.get(k, 0) < v:
                deps[k] = v

        for a in reads:
            for p in a.parts:
                for k, v in a.tile.w[p].items():
                    add(k, v, True)
        for a in writes:
            for p in a.parts:
                for k, v in a.tile.w[p].items():
                    add(k, v, False)
                for k, v in a.tile.r[p].items():
                    add(k, v, False)
        out = []
        for k, v in deps.items():
            if self.seen[eng].get(k, 0) < v:
                self.seen[eng][k] = v
                out.append((k, v))
        return out

    def _commit(self, key, val, reads, writes):
        for a in reads:
            for p in a.parts:
                d = a.tile.r[p]
                if d.get(key, 0) < val:
                    d[key] = val
        for a in writes:
            for p in a.parts:
                a.tile.w[p] = {key: val}
                a.tile.r[p] = {}

    def op(self, eng, fn, reads=(), writes=()):
        reads = [_acc(x) for x in reads]
        writes = [_acc(x) for x in writes]
        waits = self._deps(eng, reads, writes)
        self.cnt[eng] += 1
        val = self.cnt[eng]
        sem = self.sems[eng]
        wl = [(self.sems[k], v) for k, v in waits]
        self.n_ops += 1
        self.n_waits += len(wl)

        def thunk(hh, fn=fn, wl=wl, sem=sem):
            for s, v in wl:
                hh.wait_ge(s, v)
            fn(hh).then_inc(sem, 1)

        self.streams[eng].append(thunk)
        self._commit(eng, val, reads, writes)

    def op_multi(self, eng, fns, reads=(), writes=()):
        reads = [_acc(x) for x in reads]
        writes = [_acc(x) for x in writes]
        waits = self._deps(eng, reads, writes)
        self.cnt[eng] += 1
        val = self.cnt[eng]
        sem = self.sems[eng]
        wl = [(self.sems[k], v) for k, v in waits]
        self.n_ops += len(fns)
        self.n_waits += len(wl)

        def thunk(hh, fns=fns, wl=wl, sem=sem):
            for s, v in wl:
                hh.wait_ge(s, v)
            for f in fns[:-1]:
                f(hh)
            fns[-1](hh).then_inc(sem, 1)

        self.streams[eng].append(thunk)
        self._commit(eng, val, reads, writes)

    def dma(self, q, out_ap, in_ap, reads=(), writes=(), **kw):
        reads = [_acc(x) for x in reads]
        writes = [_acc(x) for x in writes]
        i = self.dma_rr[q]
        self.dma_rr[q] = (i + 1) % self.NDMA_SEMS
        key = f"d_{q}{i}"
        waits = self._deps(q, reads, writes, strict=True)
        prev = self.cnt[key]
        if prev > 0 and self.seen[q].get(key, 0) < prev:
            self.seen[q][key] = prev
            waits.append((key, prev))
        self.cnt[key] += 16
        val = self.cnt[key]
        sem = self.sems[key]
        wl = [(self.sems[k], v) for k, v in waits]
        self.n_ops += 1
        self.n_waits += len(wl)

        def thunk(hh, wl=wl, sem=sem, out_ap=out_ap, in_ap=in_ap, kw=kw):
            for s, v in wl:
                hh.wait_ge(s, v)
            hh.dma_start(out=out_ap, in_=in_ap, **kw).then_inc(sem, 16)

        self.streams[q].append(thunk)
        self._commit(key, val, reads, writes)
        return key, val

    def final_wait(self, eng, tiles):
        ev = {}
        for t in tiles:
            for k, v in t.all_events().items():
                if ev.get(k, 0) < v:
                    ev[k] = v
        wl = [(self.sems[k], v) for k, v in ev.items()]

        def thunk(hh, wl=wl):
            for s, v in wl:
                hh.wait_ge(s, v)

        self.streams[eng].append(thunk)

    def emit(self):
        nc = self.nc
        with nc.Block() as block:
            def mk(name):
                def f(hh):
                    for t in self.streams[name]:
                        t(hh)
                return f
            block.tensor(mk("pe"))
            block.scalar(mk("act"))
            block.vector(mk("dve"))
            block.gpsimd(mk("pool"))
            block.sync(mk("sp"))


class Mem:
    def __init__(self, nc, sbuf_bytes):
        self.nc = nc
        self.sb_raw = nc.alloc_sbuf_tensor("sb_raw", [128, sbuf_bytes // 4], F32)
        self.ps_raw = nc.alloc_psum_tensor("ps_raw", [128, 4096], F32)
        self.sbuf_bytes = sbuf_bytes
        self.live = {"sb": [], "ps": []}

    def _place(self, space, name, off, nbytes, dtype, shape, nparts, parts_total=128):
        raw = self.sb_raw if space == "sb" else self.ps_raw
        assert off % 4 == 0 and nbytes % 4 == 0, (name, off, nbytes)
        ap = raw[0:parts_total, off // 4:(off + nbytes) // 4]
        if dtype != F32:
            ap = ap.bitcast(dtype)
        if len(shape) == 2:
            ap = ap.rearrange("p (a b) -> p a b", a=shape[0], b=shape[1])
        elif len(shape) == 3:
            ap = ap.rearrange("p (a b c) -> p a b c", a=shape[0], b=shape[1], c=shape[2])
        t = Tile(name, space, off, nbytes, ap, nparts)
        keep = []
        for o in self.live[space]:
            if o.off < off + nbytes and off < o.off + o.nbytes:
                t.inherit(o.all_events())
                if not (off <= o.off and o.off + o.nbytes <= off + nbytes):
                    keep.append(o)
            else:
                keep.append(o)
        keep.append(t)
        self.live[space] = keep
        return t

    def sb(self, name, off, dtype, shape, nparts=1):
        n = int(np.prod(shape)) * DT_SIZE[dtype]
        assert off + n <= self.sbuf_bytes, (name, off, n, self.sbuf_bytes)
        return self._place("sb", name, off, n, dtype, list(shape), nparts)

    def ps(self, name, bank, dtype, shape, nparts=1, nbanks=1, byte_off=0):
        n = int(np.prod(shape)) * DT_SIZE[dtype]
        assert n + byte_off <= 2048 * nbanks
        return self._place("ps", name, bank * 2048 + byte_off, n, dtype, list(shape), nparts)


D = 1024
DFF = 2816
NM = 44
NMH = 22
KC = 8
PD = 256
HD = 64
EPS = 1e-6
SBUF_BYTES = 207 * 1024

INPUT_SHAPES = {
    "norm_mix": [2, D], "norm_ffn": [2, D], "norm_ple": [2, D], "norm_kv": [D], "norm_final": [D],
    "a_w_in": [1, D, 2 * D], "a_norm_v": [1, D], "a_w_s": [1, 8, 128, 128], "a_b_s": [1, 8, 128],
    "a_w_out": [1, D, D], "w_kv": [D, 512], "b_w_q": [1, D, D], "b_sinks": [1, 16], "b_w_o": [1, D, D],
    "f_w_up": [2, D, 2 * DFF], "f_conv_w": [2, 3, 2 * DFF], "f_conv_b": [2, 2 * DFF], "f_w_down": [2, DFF, D],
    "ple_w_in": [2, PD, D], "ple_w_gate": [2, D, D], "ple_b_gate": [2, D],
}


def hL(c):
    return 8 * (c // 4) + (c % 4)


def hU(c):
    return hL(c) + 4


class Bump:
    def __init__(self, base, size):
        self.base, self.size, self.p = base, size, base

    def take(self, n):
        n = (n + 31) // 32 * 32
        o = self.p
        self.p += n
        assert self.p <= self.base + self.size, ("sbuf overflow", self.p, self.base + self.size)
        return o


class Ring:
    def __init__(self, S, M, base, size, schedule):
        self.S, self.M, self.base, self.size = S, M, base, size
        self.schedule = schedule
        self.next = 0
        self.ptr = 0
        self.live = []
        self.head = 0

    def _try_place(self, n):
        unrel = [(o, nb) for (_, _, o, nb, rel) in self.live if not rel]
        cands = [self.ptr, 0] + sorted(o + nb for (o, nb) in unrel)
        for off in cands:
            if off + n > self.size:
                continue
            if all(not (o < off + n and off < o + nb) for (o, nb) in unrel):
                return off
        return None

    def prefetch(self, limit=None):
        cnt = 0
        while self.next < len(self.schedule):
            name, dt_, dap, shape = self.schedule[self.next]
            n = (int(np.prod(shape)) * 2 + 31) // 32 * 32
            off = self._try_place(n)
            if off is None:
                break
            self.live = [e for e in self.live if not (e[4] and e[2] < off + n and off < e[2] + e[3])]
            t = self.M.sb("w_" + str(name), self.base + off, BF16, shape)
            self.S.dma("sp", t.ap, dap, reads=[dt_], writes=[t])
            self.live.append([name, t, off, n, False])
            self.ptr = off + n
            self.next += 1
            cnt += 1
            if limit is not None and cnt >= limit:
                break

    def get(self, name):
        for e in self.live:
            if e[0] == name and not e[4]:
                return e[1]
        self.prefetch()
        for e in self.live:
            if e[0] == name and not e[4]:
                return e[1]
        raise RuntimeError(f"ring: piece {name} not loadable (ring too small?) next={self.schedule[self.next][0] if self.next < len(self.schedule) else None}")

    def release(self, name):
        for e in self.live:
            if e[0] == name and not e[4]:
                e[4] = True
                break
        else:
            raise RuntimeError(f"ring release: {name} not live")
        self.prefetch()


def build_program(S_core, T=512):
    NCH = T // 128
    NT = S_core // T
    assert NT * T == S_core
    nc = bass.Bass("TRN2", target_bir_lowering=False)

    def din(name, shape):
        return nc.dram_tensor(name, list(shape), F32, kind="ExternalInput").ap()

    x_d = din("x", [S_core, D])
    p_d = din("p", [2, S_core, PD])
    ident_d = din("ident", [128, 128])
    W = {k: din(k, v) for k, v in INPUT_SHAPES.items()}
    out_d = nc.dram_tensor("out", [S_core, D], F32, kind="ExternalOutput").ap()

    S = Sched(nc, dma_queues=("sp", "act", "pool"))
    M = Mem(nc, SBUF_BYTES)

    def dram_scratch(name, shape, nparts):
        ap = nc.dram_tensor(name, [128] + list(shape), BF16, kind="Internal").ap()
        return Tile(name, "dram", 0, 0, ap, nparts)

    s_win_u = dram_scratch("s_win_u", [4, 8, 256], 8)
    s_win_v = dram_scratch("s_win_v", [8, 1024], 8)
    s_wout = dram_scratch("s_wout", [8, 1024], 8)
    s_wup = [dram_scratch(f"s_wup{l}", [11, 8, 512], 16) for l in range(2)]
    s_wdown = [dram_scratch(f"s_wdown{l}", [22, 1024], 22) for l in range(2)]
    s_wgate = [dram_scratch(f"s_wgate{l}", [8, 1024], 8) for l in range(2)]
    s_plein = [dram_scratch(f"s_plein{l}", [2, 1024], 2) for l in range(2)]
    s_wkv = dram_scratch("s_wkv", [8, 512], 8)
    s_wq = dram_scratch("s_wq", [4, 8, 256], 8)
    s_wo = dram_scratch("s_wo", [8, 1024], 8)
    out_t = Tile("out", "dram", 0, 0, out_d, S_core // 128)

    A = Bump(0, SBUF_BYTES)

    def sbt(name, dtype, shape, nparts=1):
        n = int(np.prod(shape)) * DT_SIZE[dtype]
        return M.sb(name, A.take(n), dtype, shape, nparts)

    identf = sbt("identf", F32, [128])
    identb = sbt("identb", BF16, [128])
    ones = sbt("ones", BF16, [128])
    negh = sbt("negh", F32, [1])
    gcols = sbt("gcols", F32, [56])
    cw = sbt("cw", F32, [2, 4, NM])
    gv_tile = sbt("gv_tile", F32, [D])
    gfin_tile = sbt("gfin_tile", F32, [D])
    wsT = sbt("wsT", BF16, [8, 128])
    bs_hl = sbt("bs_hl", BF16, [2, D])
    bg_hl = sbt("bg_hl", BF16, [2, 2, D])
    sinkcol = sbt("sinkcol", F32, [2, 4])
    sinkt = sbt("sinkt", F32, [2, 512])
    Mk = sbt("Mk", BF16, [2, 4, 512])
    halo = [sbt(f"halo{l}", F32, [NM, 2]) for l in range(2)]
    corr = sbt("corr", F32, [NM, 2])
    ctmp = sbt("ctmp", F32, [2, NM])
    kT = sbt("kT", BF16, [2, 128 + T], nparts=NCH + 1)
    vpad = sbt("vpad", BF16, [NCH + 1, 4, 128], nparts=NCH + 1)
    ones_eo = sbt("ones_eo", BF16, [2, 128])
    stats = [[sbt(f"st{n}_{c}", F32, [4]) for c in range(NCH)] for n in range(7)]
    vstat = [sbt(f"vst{c}", F32, [4]) for c in range(NCH)]
    hA = sbt("hA", F32, [NCH, D], nparts=NCH)
    hB = sbt("hB", F32, [NCH, D], nparts=NCH)
    hs = [hA, hB]
    xn_a = sbt("xn_a", BF16, [KC, T], nparts=NCH)
    xn_b = sbt("xn_b", BF16, [KC, T], nparts=NCH)
    pT = [sbt(f"pT{l}", BF16, [2, T]) for l in range(2)]
    xn_tok = [sbt(f"xn_tok{i}", BF16, [D]) for i in range(3)]
    junk = sbt("junk", BF16, [D])

    scratch_base = A.take(0)
    SCRATCH = 38912
    A.take(SCRATCH)
    ring_base = A.take(0)
    ring_size = (SBUF_BYTES - ring_base) // 32 * 32

    class Phase:
        def __init__(self):
            self.b = Bump(scratch_base, SCRATCH)

        def t(self, name, dtype, shape, nparts=1):
            n = int(np.prod(shape)) * DT_SIZE[dtype]
            return M.sb(name, self.b.take(n), dtype, shape, nparts)

    psr = {"n": 0, "gen": 0, "S": 0, "W": 0, "OD": 0}
    ALLB = [0, 1, 2, 3, 4, 5, 6, 7]

    def ps_from(key, banks, name, dtype, shape, nbanks=1):
        i = psr[key] % len(banks)
        psr[key] += 1
        psr["n"] += 1
        return M.ps(f"{name}_{psr['n']}", banks[i], dtype, shape, nbanks=nbanks)

    def ps_narrow(name, dtype, shape):
        return ps_from("gen", ALLB, name, dtype, shape)

    def ps_wide(name, dtype, shape, nparts=1):
        if psr["gen"] % 2:
            psr["gen"] += 1
        b = psr["gen"] % 8
        psr["gen"] += 2
        psr["n"] += 1
        return M.ps(f"{name}_{psr['n']}", b, dtype, shape, nparts=nparts, nbanks=2)

    pro = Bump(scratch_base, SBUF_BYTES - scratch_base)

    def ptmp(name, dtype, shape):
        n = int(np.prod(shape)) * DT_SIZE[dtype]
        nn = (n + 31) // 32 * 32
        if pro.p + nn > pro.base + pro.size:
            pro.p = pro.base
        return M.sb(name, pro.take(n), dtype, shape)

    S.dma("sp", identf.ap, ident_d, writes=[identf])
    S.op("dve", lambda e: e.tensor_copy(identb.ap, identf.ap), reads=[identf], writes=[identb])
    S.op("pool", lambda e: e.memset(ones.ap, 1.0), writes=[ones])
    S.op("pool", lambda e: e.memset(negh.ap, -0.5), writes=[negh])
    S.op("pool", lambda e: e.memset(vpad.ap, 0.0), writes=[vpad])
    S.op("pool", lambda e: e.memset(ones_eo.ap, 0.0), writes=[ones_eo])
    S.op("pool", lambda e: e.memset(ones_eo.ap[:, 0, 0:64], 1.0), writes=[ones_eo])
    S.op("pool", lambda e: e.memset(ones_eo.ap[:, 1, 64:128], 1.0), writes=[ones_eo])
    for l in range(2):
        S.op("pool", lambda e, l=l: e.memset(halo[l].ap, 0.0), writes=[halo[l]])

    gst = ptmp("gst", F32, [128])
    gvecs = [W["norm_mix"][0], W["norm_ffn"][0], W["norm_ple"][0], W["norm_kv"], W["norm_mix"][1],
             W["norm_ffn"][1], W["norm_ple"][1]]
    G_MIX0, G_FFN0, G_PLE0, G_KV, G_MIX1, G_FFN1, G_PLE1 = range(7)
    for v, g in enumerate(gvecs):
        S.dma("sp", gst.ap[v * 8:(v + 1) * 8, :], g.rearrange("(k p) -> k p", p=128), writes=[gst])
    pt = ps_narrow("gT", F32, [56])
    S.op("pe", lambda e: e.transpose(pt.ap, gst.ap[0:56, :], identf.ap[0:56, 0:56]), reads=[gst, identf], writes=[pt])
    S.op("dve", lambda e: e.tensor_copy(gcols.ap, pt.ap), reads=[pt], writes=[gcols])

    def gcol(v, k):
        return gcols.ap[:, v * 8 + k:v * 8 + k + 1]

    for l in range(2):
        for pair, (va, vb) in enumerate([(W["f_conv_w"][l, 0], W["f_conv_w"][l, 1]),
                                         (W["f_conv_w"][l, 2], W["f_conv_b"][l])]):
            cst = ptmp("cst", F32, [128])
            S.dma("sp", cst.ap[0:44, :], va.rearrange("(m p) -> m p", p=128), writes=[cst])
            S.dma("sp", cst.ap[44:88, :], vb.rearrange("(m p) -> m p", p=128), writes=[cst])
            pc = ps_narrow("cT", F32, [88])
            S.op("pe", lambda e, pc=pc, cst=cst: e.transpose(pc.ap, cst.ap[0:88, :], identf.ap[0:88, 0:88]),
                 reads=[cst, identf], writes=[pc])
            S.op("dve", lambda e, pc=pc, l=l, pair=pair: e.tensor_copy(
                cw.ap[:, l, 2 * pair:2 * pair + 2, :], pc.ap.rearrange("p (a b) -> p a b", a=2)),
                reads=[pc], writes=[cw])

    def cwc(l, which, m):
        return cw.ap[:, l, which, m:m + 1]

    S.dma("sp", gv_tile.ap, W["a_norm_v"].broadcast_to([128, D]), writes=[gv_tile])
    S.dma("sp", gfin_tile.ap, W["norm_final"].rearrange("(o n) -> o n", o=1).broadcast_to([128, D]), writes=[gfin_tile])

    dist_i = ptmp("dist_i", I32, [128])
    dist = ptmp("dist", F32, [128])
    tri = ptmp("tri", F32, [128])
    ntri = ptmp("ntri", F32, [128])
    distc = ptmp("distc", F32, [128])
    dist128 = ptmp("dist128", F32, [128])
    S.op("pool", lambda e: e.iota(dist_i.ap, [[1, 128]], base=0, channel_multiplier=-1), writes=[dist_i])
    S.op("dve", lambda e: e.tensor_copy(dist.ap, dist_i.ap), reads=[dist_i], writes=[dist])
    S.op("dve", lambda e: e.tensor_scalar(tri.ap, dist.ap, 0.0, None, ALU.is_ge), reads=[dist], writes=[tri])
    S.op("dve", lambda e: e.tensor_scalar(ntri.ap, dist.ap, 0.0, None, ALU.is_lt), reads=[dist], writes=[ntri])
    S.op("dve", lambda e: e.tensor_scalar(distc.ap, dist.ap, 0.0, None, ALU.max), reads=[dist], writes=[distc])
    S.op("dve", lambda e: e.tensor_scalar(dist128.ap, dist.ap, 128.0, None, ALU.add), reads=[dist], writes=[dist128])

    wst = ptmp("wst", F32, [8, 128])
    S.dma("sp", wst.ap, W["a_w_s"][0].rearrange("h t s -> t h s"), writes=[wst])
    for hh in range(8):
        pw_ = ps_narrow("wsTp", F32, [128])
        S.op("pe", lambda e, pw_=pw_, hh=hh: e.transpose(pw_.ap, wst.ap[:, hh, :], identf.ap), reads=[wst, identf], writes=[pw_])
        S.op("dve", lambda e, pw_=pw_, hh=hh: e.tensor_tensor(wsT.ap[:, hh, :], pw_.ap, tri.ap, ALU.mult),
             reads=[pw_, tri], writes=[wsT])

    for g in range(4):
        for r in range(4):
            slope = float(2.0 ** (-8.0 * (4 * g + r + 1) / 16.0))
            tc_ = ptmp("mtc", F32, [128])
            tp_ = ptmp("mtp", F32, [128])
            S.op("act", lambda e, tc_=tc_, slope=slope: e.activation(tc_.ap, distc.ap, AF.Exp, scale=-slope), reads=[distc], writes=[tc_])
            S.op("dve", lambda e, tc_=tc_, g=g, r=r: e.tensor_tensor(Mk.ap[:, 1, g, r * 128:(r + 1) * 128], tc_.ap, tri.ap, ALU.mult),
                 reads=[tc_, tri], writes=[Mk])
            S.op("act", lambda e, tp_=tp_, slope=slope: e.activation(tp_.ap, dist128.ap, AF.Exp, scale=-slope), reads=[dist128], writes=[tp_])
            S.op("dve", lambda e, tp_=tp_, g=g, r=r: e.tensor_tensor(Mk.ap[:, 0, g, r * 128:(r + 1) * 128], tp_.ap, ntri.ap, ALU.mult),
                 reads=[tp_, ntri], writes=[Mk])

    def hilo(src_ap, n, hi_ap, lo_ap, src_tile, dst_tile):
        hf = ptmp("hf", F32, [n])
        lf = ptmp("lf", F32, [n])
        S.op("dve", lambda e: e.tensor_copy(hi_ap, src_ap), reads=[src_tile], writes=[dst_tile])
        S.op("dve", lambda e: e.tensor_copy(hf.ap[0:1, :], hi_ap), reads=[dst_tile], writes=[hf])
        S.op("dve", lambda e: e.tensor_tensor(lf.ap[0:1, :], src_ap, hf.ap[0:1, :], ALU.subtract), reads=[src_tile, hf], writes=[lf])
        S.op("dve", lambda e: e.tensor_copy(lo_ap, lf.ap[0:1, :]), reads=[lf], writes=[dst_tile])

    for half in range(2):
        src = W["b_sinks"].rearrange("o (jp gi r) -> o jp gi r", jp=2, gi=2, r=4)[:, :, half, :]
        S.dma("sp", sinkcol.ap[half * 64:(half + 1) * 64, :, :], src.broadcast_to([64, 2, 4]), writes=[sinkcol])
    S.op("act", lambda e: e.activation(sinkcol.ap, sinkcol.ap, AF.Exp), reads=[sinkcol], writes=[sinkcol])
    S.op("dve", lambda e: e.tensor_copy(sinkt.ap.rearrange("p j (r i) -> p j r i", r=4),
                                        sinkcol.ap.unsqueeze(3).broadcast_to([128, 2, 4, 128])), reads=[sinkcol], writes=[sinkt])
    bsr = ptmp("bsr", F32, [D])
    S.dma("sp", bsr.ap[0:1, :], W["a_b_s"].rearrange("o h t -> o (h t)"), writes=[bsr])
    hilo(bsr.ap[0:1, :], D, bs_hl.ap[0:1, 0, :], bs_hl.ap[0:1, 1, :], bsr, bs_hl)
    for l in range(2):
        bgr = ptmp("bgr", F32, [D])
        S.dma("sp", bgr.ap[0:1, :], W["ple_b_gate"][l:l + 1, :], writes=[bgr])
        hilo(bgr.ap[0:1, :], D, bg_hl.ap[0:1, l, 0, :], bg_hl.ap[0:1, l, 1, :], bgr, bg_hl)

    rot = {"i": 0}

    def cast(dst_ap, src_ap, scale_ap, reads, writes):
        i = rot["i"]
        rot["i"] += 1
        if i % 2 == 0:
            sc = scale_ap if scale_ap is not None else 1.0
            S.op("act", lambda e: e.activation(dst_ap, src_ap, AF.Copy, scale=sc), reads=reads, writes=writes)
        else:
            sc = scale_ap if scale_ap is not None else 1.0
            S.op("dve", lambda e: e.tensor_scalar(dst_ap, src_ap, sc, None, ALU.mult), reads=reads, writes=writes)

    def prep_plain(Wap, KD, N, gain, dst, dst_fn, cb=None, rows_fn=None, perm_q=False):
        cb = cb or N
        for kb in range(KD // 128):
            for ci in range(N // cb):
                st = ptmp("wst_", F32, [cb])
                if rows_fn is None:
                    S.dma("sp", st.ap, Wap[kb * 128:(kb + 1) * 128, ci * cb:(ci + 1) * cb], writes=[st])
                else:
                    for (p0, r0, nr) in rows_fn(kb):
                        S.dma("sp", st.ap[p0:p0 + nr, :], Wap[r0:r0 + nr, ci * cb:(ci + 1) * cb], writes=[st])
                so = ptmp("wso_", BF16, [cb])
                sc = gcol(gain, kb) if gain is not None else None
                if not perm_q:
                    cast(so.ap, st.ap, sc, [st, gcols], [so])
                else:
                    for half in range(2):
                        o_ap = so.ap.rearrange("p (c x) -> p c x", x=128)[:, :, half * 64:(half + 1) * 64].rearrange("p (a r) d -> p a r d", a=2)
                        i_ap = st.ap.rearrange("p (a hh d) -> p a hh d", a=2, hh=8)[:, :, 4 * half:4 * half + 4, :]
                        S.op("act", lambda e, o_ap=o_ap, i_ap=i_ap, sc=sc: e.activation(o_ap, i_ap, AF.Copy, scale=sc),
                             reads=[st, gcols], writes=[so])
                for (d_ap, s_ap, part) in dst_fn(kb, ci, so):
                    S.dma("pool", d_ap, s_ap, reads=[so], writes=[(dst, part)])

    a_w_in = W["a_w_in"][0]
    for kb in range(KC):
        st = ptmp("wst_", F32, [2 * D])
        S.dma("sp", st.ap, a_w_in[kb * 128:(kb + 1) * 128, :], writes=[st])
        so = ptmp("wso_", BF16, [2 * D])
        cast(so.ap, st.ap, gcol(G_MIX0, kb), [st, gcols], [so])
        S.dma("pool", s_win_u.ap[:, :, kb, :], so.ap[:, 0:D].rearrange("p (j x) -> p j x", j=4), reads=[so], writes=[(s_win_u, kb)])
        S.dma("pool", s_win_v.ap[:, kb, :], so.ap[:, D:2 * D], reads=[so], writes=[(s_win_v, kb)])
    prep_plain(W["a_w_out"][0], D, D, None, s_wout, lambda kb, ci, so: [(s_wout.ap[:, kb, :], so.ap, kb)])
    for l in range(2):
        prep_plain(W["f_w_up"][l], D, 2 * DFF, G_FFN0 if l == 0 else G_FFN1, s_wup[l],
                   lambda kb, ci, so, l=l: [(s_wup[l].ap[:, :, kb, ci * 256:(ci + 1) * 256],
                                             so.ap.rearrange("p (i x) -> p i x", i=11), kb * 2 + ci)], cb=DFF)
        prep_plain(W["f_w_down"][l], DFF, D, None, s_wdown[l], lambda kb, ci, so, l=l: [(s_wdown[l].ap[:, kb, :], so.ap, kb)])
        prep_plain(W["ple_w_gate"][l], D, D, G_PLE0 if l == 0 else G_PLE1, s_wgate[l],
                   lambda kb, ci, so, l=l: [(s_wgate[l].ap[:, kb, :], so.ap, kb)])
        prep_plain(W["ple_w_in"][l], PD, D, None, s_plein[l], lambda kb, ci, so, l=l: [(s_plein[l].ap[:, kb, :], so.ap, kb)])
    prep_plain(W["w_kv"], D, 512, G_KV, s_wkv, lambda kb, ci, so: [(s_wkv.ap[:, kb, :], so.ap, kb)])
    prep_plain(W["b_w_q"][0], D, D, G_MIX1, s_wq,
               lambda kb, ci, so: [(s_wq.ap[:, :, kb, :], so.ap.rearrange("p (j x) -> p j x", j=4), kb)], perm_q=True)
    prep_plain(W["b_w_o"][0], D, D, None, s_wo, lambda kb, ci, so: [(s_wo.ap[:, kb, :], so.ap, kb)],
               rows_fn=lambda c: [(0, hL(c) * 64, 64), (64, hU(c) * 64, 64)])

    WD_K = [0, 6, 11, 17, 22]
    WD_AFTER = {1: 0, 3: 1, 5: 2, 7: 3}
    sched = []
    for ti in range(NT):
        for j in range(4):
            sched.append((("win_u", ti, j), s_win_u, s_win_u.ap[:, j], [8, 256]))
        sched.append((("win_v", ti), s_win_v, s_win_v.ap, [8, 1024]))
        sched.append((("wout", ti), s_wout, s_wout.ap, [8, 1024]))
        for l in range(2):
            if l == 1:
                sched.append((("wkv", ti), s_wkv, s_wkv.ap, [8, 512]))
                for j in range(4):
                    sched.append((("wq", ti, j), s_wq, s_wq.ap[:, j], [8, 256]))
                sched.append((("wo", ti), s_wo, s_wo.ap, [8, 1024]))
            for i in range(11):
                sched.append((("wup", ti, l, i), s_wup[l], s_wup[l].ap[:, i], [8, 512]))
            for q in range(4):
                k0, k1 = WD_K[q], WD_K[q + 1]
                sched.append((("wdown", ti, l, q), s_wdown[l], s_wdown[l].ap[:, k0:k1], [k1 - k0, 1024]))
            sched.append((("wgate", ti, l), s_wgate[l], s_wgate[l].ap, [8, 1024]))
            sched.append((("plein", ti, l), s_plein[l], s_plein[l].ap, [2, 1024]))
    ring = Ring(S, M, ring_base, ring_size, sched)

    def mm_group(specs, reads, writes):
        fns = [(lambda e, o=o, l_=l_, r_=r_, st=st, sp=sp: e.matmul(o, lhsT=l_, rhs=r_, start=st, stop=sp))
               for (o, l_, r_, st, sp) in specs]
        S.op_multi("pe", fns, reads=reads, writes=writes)

    xtok_i = {"i": 0}

    def make_phases(h):
        def rstd_chain(src_ap, src_acc, st):
            S.op("act", lambda e: e.activation(junk.ap, src_ap, AF.Square, accum_out=st.ap[:, 0:1]), reads=[src_acc], writes=[junk, st])
            S.op("dve", lambda e: e.tensor_scalar(st.ap[:, 1:2], st.ap[:, 0:1], 1.0 / D, EPS, ALU.mult, ALU.add), reads=[st], writes=[st])
            S.op("pool", lambda e: e.tensor_tensor(st.ap[:, 2:3], st.ap[:, 1:2], negh.ap, ALU.pow), reads=[st, negh], writes=[st])

        def norm_a(c, st):
            rstd_chain(h.ap[:, c, :], (h, c), st)
            xt = xn_tok[xtok_i["i"] % len(xn_tok)]
            xtok_i["i"] += 1
            S.op("act", lambda e: e.activation(xt.ap, h.ap[:, c, :], AF.Copy, scale=st.ap[:, 2:3]), reads=[(h, c), st], writes=[xt])
            return xt

        def norm_b(c, xn, xt, tp_alloc=None):
            tp = (tp_alloc or ps_narrow)("tp", BF16, [KC, 128])
            fns = [(lambda e, k=k: e.transpose(tp.ap[:, k, :], xt.ap[:, k * 128:(k + 1) * 128], identb.ap)) for k in range(KC)]
            S.op_multi("pe", fns, reads=[xt, identb], writes=[tp])
            S.op("dve", lambda e: e.tensor_copy(xn.ap[:, :, c * 128:(c + 1) * 128], tp.ap), reads=[tp], writes=[(xn, c)])

        def norm_chunk(c, xn, st):
            norm_b(c, xn, norm_a(c, st))

        class Deferred:
            def __init__(self):
                self.q = []

            def push(self, fn):
                self.q.append(fn)

            def flush(self, keep=0):
                while len(self.q) > keep:
                    self.q.pop(0)()

        def final_chunk(ti, c, st, ot):
            rstd_chain(h.ap[:, c, :], (h, c), st)
            S.op("dve", lambda e: e.scalar_tensor_tensor(ot.ap, h.ap[:, c, :], st.ap[:, 2:3], gfin_tile.ap, ALU.mult, ALU.mult),
                 reads=[(h, c), st, gfin_tile], writes=[ot])
            r0 = ti * T + c * 128
            S.dma("pool", out_d[r0:r0 + 128, :], ot.ap, reads=[ot], writes=[(out_t, r0 // 128)])

        def load_p(ti, layers=(0, 1)):
            for l in layers:
                ph = Phase()
                pst = ph.t("pst", F32, [NCH, PD])
                pb = ph.t("pb", BF16, [NCH, PD])
                S.dma("sp", pst.ap, p_d[l, ti * T:(ti + 1) * T, :].rearrange("(c p) d -> p c d", p=128), writes=[pst])
                S.op("pool", lambda e, pb=pb, pst=pst: e.tensor_copy(pb.ap, pst.ap), reads=[pst], writes=[pb])
                ptp = ps_narrow("ptp", BF16, [2, T])
                fns = [(lambda e, c=c, j=j, ptp=ptp, pb=pb: e.transpose(ptp.ap[:, j, c * 128:(c + 1) * 128], pb.ap[:, c, j * 128:(j + 1) * 128], identb.ap))
                       for c in range(NCH) for j in range(2)]
                S.op_multi("pe", fns, reads=[pb, identb], writes=[ptp])
                S.op("dve", lambda e, ptp=ptp, l=l: e.tensor_copy(pT[l].ap, ptp.ap), reads=[ptp], writes=[pT[l]])

        def mixer_a(ti, xn, xn_next):
            ph = Phase()
            uT = ph.t("uT", F32, [KC, T], nparts=KC)
            usT = ph.t("usT", BF16, [KC, T], nparts=NCH)
            gvs = [ph.t(f"gv{i}", F32, [D]) for i in range(2)]
            vns = [ph.t(f"vn{i}", BF16, [D]) for i in range(2)]
            for j in range(4):
                wu = ring.get(("win_u", ti, j))
                for cc in range(2):
                    m = 2 * j + cc
                    pu = ps_narrow("pu", F32, [T])
                    mm_group([(pu.ap, wu.ap[:, k, cc * 128:(cc + 1) * 128], xn.ap[:, k, :], k == 0, k == KC - 1) for k in range(KC)],
                             reads=[wu, xn], writes=[pu])
                    S.op("act", lambda e, pu=pu, m=m: e.activation(uT.ap[:, m, :], pu.ap, AF.Gelu_apprx_tanh), reads=[pu], writes=[(uT, m)])
                ring.release(("win_u", ti, j))
            wv = ring.get(("win_v", ti))
            wo = ring.get(("wout", ti))
            state = {}

            def emit_v(c):
                pv = ps_wide("pv", F32, [D])
                mm_group([(pv.ap[:, hf * 512:(hf + 1) * 512], xn.ap[:, k, c * 128:(c + 1) * 128], wv.ap[:, k, hf * 512:(hf + 1) * 512], k == 0, k == KC - 1)
                          for hf in range(2) for k in range(KC)], reads=[(xn, c), wv], writes=[pv])
                gv = gvs[c % 2]
                vn = vns[c % 2]
                st = vstat[c]
                S.op("act", lambda e: e.activation(gv.ap, pv.ap, AF.Gelu_apprx_tanh), reads=[pv], writes=[gv])
                rstd_chain(gv.ap, gv, st)
                S.op("dve", lambda e: e.scalar_tensor_tensor(vn.ap, gv.ap, st.ap[:, 2:3], gv_tile.ap, ALU.mult, ALU.mult),
                     reads=[gv, st, gv_tile], writes=[vn])
                state[c] = vn

            def emit_s(c):
                vn = state[c]
                ps_ = ps_wide("ps", F32, [8, 128])
                specs = []
                for hh in range(8):
                    specs.append((ps_.ap[:, hh, :], vn.ap[:, hh * 128:(hh + 1) * 128], wsT.ap[:, hh, :], True, False))
                    specs.append((ps_.ap[:, hh, :], ones.ap[0:1, 0:128], bs_hl.ap[0:1, 0, hh * 128:(hh + 1) * 128], False, False))
                    specs.append((ps_.ap[:, hh, :], ones.ap[0:1, 0:128], bs_hl.ap[0:1, 1, hh * 128:(hh + 1) * 128], False, True))
                mm_group(specs, reads=[vn, wsT, ones, bs_hl], writes=[ps_])
                S.op("dve", lambda e: e.tensor_tensor(usT.ap[:, :, c * 128:(c + 1) * 128], ps_.ap, uT.ap[:, :, c * 128:(c + 1) * 128], ALU.mult),
                     reads=[ps_, uT], writes=[(usT, c)])

            def emit_o(c):
                po = ps_wide("po", F32, [D])
                mm_group([(po.ap[:, hf * 512:(hf + 1) * 512], usT.ap[:, k, c * 128:(c + 1) * 128], wo.ap[:, k, hf * 512:(hf + 1) * 512], k == 0, k == KC - 1)
                          for hf in range(2) for k in range(KC)], reads=[(usT, c), wo], writes=[po])
                S.op("dve", lambda e: e.tensor_tensor(h.ap[:, c, :], po.ap, h.ap[:, c, :], ALU.add), reads=[po, (h, c)], writes=[(h, c)])
                xt = norm_a(c, stats[1][c])
                dfr.push(lambda: norm_b(c, xn_next, xt))

            dfr = Deferred()
            for i in range(NCH + 2):
                if i < NCH:
                    emit_v(i)
                if 1 <= i <= NCH:
                    emit_s(i - 1)
                if i >= 2:
                    emit_o(i - 2)
                    dfr.flush(keep=1)
            dfr.flush()
            ring.release(("win_v", ti))
            ring.release(("wout", ti))

        def ffn(ti, l, xn, xn_next, st_idx):
            ph = Phase()
            actT = ph.t("actT", BF16, [NMH, T], nparts=NMH)
            atl = [ph.t(f"a{i}", F32, [T]) for i in range(8)]
            ai = {"i": 0}
            hl = halo[l]
            S.op("dve", lambda e: e.tensor_tensor(ctmp.ap[:, 0, :], hl.ap[:, :, 1], cw.ap[:, l, 1, :], ALU.mult), reads=[hl, cw], writes=[ctmp])
            S.op("dve", lambda e: e.tensor_tensor(ctmp.ap[:, 1, :], hl.ap[:, :, 0], cw.ap[:, l, 0, :], ALU.mult), reads=[hl, cw], writes=[ctmp])
            S.op("dve", lambda e: e.tensor_tensor(corr.ap[:, :, 0], ctmp.ap[:, 0, :], ctmp.ap[:, 1, :], ALU.add), reads=[ctmp], writes=[corr])
            S.op("dve", lambda e: e.tensor_tensor(corr.ap[:, :, 1], hl.ap[:, :, 1], cw.ap[:, l, 0, :], ALU.mult), reads=[hl, cw], writes=[corr])
            def stage1(w, mp, cc):
                res = []
                phs = []
                for part in range(2):
                    col0 = part * 256 + cc * 128
                    pH = ps_narrow("pH", F32, [T])
                    mm_group([(pH.ap, w.ap[:, k, col0:col0 + 128], xn.ap[:, k, :], k == 0, k == KC - 1) for k in range(KC)],
                             reads=[w, xn], writes=[pH])
                    phs.append(pH)
                for part in range(2):
                    m = mp + NMH * part
                    pH = phs[part]
                    a = atl[ai["i"] % len(atl)]
                    ai["i"] += 1
                    S.op("act", lambda e, a=a, pH=pH, m=m: e.activation(a.ap, pH.ap, AF.Identity, scale=cwc(l, 2, m), bias=cwc(l, 3, m)),
                         reads=[pH, cw], writes=[a])
                    res.append(a)
                    S.op("act", lambda e, pH=pH, m=m: e.activation(hl.ap[:, m, :], pH.ap[:, T - 2:T], AF.Copy), reads=[pH], writes=[hl])
                for part in range(2):
                    m = mp + NMH * part
                    pH = phs[part]
                    a = res[part]
                    S.op("dve", lambda e, a=a, pH=pH, m=m: e.scalar_tensor_tensor(a.ap[:, 1:T], pH.ap[:, 0:T - 1], cwc(l, 1, m), a.ap[:, 1:T], ALU.mult, ALU.add),
                         reads=[pH, cw, a], writes=[a])
                    S.op("dve", lambda e, a=a, pH=pH, m=m: e.scalar_tensor_tensor(a.ap[:, 2:T], pH.ap[:, 0:T - 2], cwc(l, 0, m), a.ap[:, 2:T], ALU.mult, ALU.add),
                         reads=[pH, cw, a], writes=[a])
                    S.op("pool", lambda e, a=a, m=m: e.tensor_tensor(a.ap[:, 0:2], a.ap[:, 0:2], corr.ap[:, m, :], ALU.add),
                         reads=[a, corr], writes=[a])
                return res

            def stage2(mp, res):
                ag, au = res
                S.op("act", lambda e: e.activation(ag.ap, ag.ap, AF.Silu), reads=[ag], writes=[ag])
                S.op("pool", lambda e: e.tensor_tensor(actT.ap[:, mp, :], ag.ap, au.ap, ALU.mult), reads=[ag, au], writes=[(actT, mp)])

            pend = []
            for i in range(11):
                w = ring.get(("wup", ti, l, i))
                for cc in range(2):
                    mp = 2 * i + cc
                    res = stage1(w, mp, cc)
                    pend.append((mp, res, i if cc == 1 else None))
                    if len(pend) > 1:
                        pmp, pres, prel = pend.pop(0)
                        stage2(pmp, pres)
                        if prel is not None:
                            ring.release(("wup", ti, l, prel))
            while pend:
                pmp, pres, prel = pend.pop(0)
                stage2(pmp, pres)
                if prel is not None:
                    ring.release(("wup", ti, l, prel))
            wds = [ring.get(("wdown", ti, l, q)) for q in range(4)]

            def wd_of(mp):
                q = max(i for i in range(4) if WD_K[i] <= mp)
                return wds[q], mp - WD_K[q]
            dfr = Deferred()
            for c in range(NCH):
                po = ps_wide("pd", F32, [D])
                specs = []
                for hf in range(2):
                    for mp in range(NMH):
                        w_, kk = wd_of(mp)
                        specs.append((po.ap[:, hf * 512:(hf + 1) * 512], actT.ap[:, mp, c * 128:(c + 1) * 128],
                                      w_.ap[:, kk, hf * 512:(hf + 1) * 512], mp == 0, mp == NMH - 1))
                mm_group(specs, reads=[actT] + wds, writes=[po])
                S.op("dve", lambda e, po=po, c=c: e.tensor_tensor(h.ap[:, c, :], po.ap, h.ap[:, c, :], ALU.add), reads=[po, (h, c)], writes=[(h, c)])
                dfr.flush()
                xt = norm_a(c, stats[st_idx][c])
                dfr.push(lambda c=c, xt=xt: norm_b(c, xn_next, xt))
            dfr.flush()
            for q in range(4):
                ring.release(("wdown", ti, l, q))

        def ple(ti, l, xn, after_chunk):
            ph = Phase()
            gates = [ph.t(f"gate{i}", F32, [D]) for i in range(2)]
            tmps = [ph.t(f"ptmp{i}", F32, [D]) for i in range(2)]
            wg = ring.get(("wgate", ti, l))
            wp = ring.get(("plein", ti, l))
            dfr = Deferred()
            for c in range(NCH):
                pg = ps_wide("pg", F32, [D])
                specs = []
                for hf in range(2):
                    o = pg.ap[:, hf * 512:(hf + 1) * 512]
                    for k in range(KC):
                        specs.append((o, xn.ap[:, k, c * 128:(c + 1) * 128], wg.ap[:, k, hf * 512:(hf + 1) * 512], k == 0, False))
                    specs.append((o, ones.ap[0:1, 0:128], bg_hl.ap[0:1, l, 0, hf * 512:(hf + 1) * 512], False, False))
                    specs.append((o, ones.ap[0:1, 0:128], bg_hl.ap[0:1, l, 1, hf * 512:(hf + 1) * 512], False, True))
                mm_group(specs, reads=[(xn, c), wg, ones, bg_hl], writes=[pg])
                gate = gates[c % 2]
                tmp = tmps[c % 2]
                S.op("act", lambda e, gate=gate, pg=pg: e.activation(gate.ap, pg.ap, AF.Sigmoid), reads=[pg], writes=[gate])
                pw = ps_wide("pw", F32, [D])
                mm_group([(pw.ap[:, hf * 512:(hf + 1) * 512], pT[l].ap[:, j, c * 128:(c + 1) * 128], wp.ap[:, j, hf * 512:(hf + 1) * 512], j == 0, j == 1)
                          for hf in range(2) for j in range(2)], reads=[pT[l], wp], writes=[pw])
                S.op("dve", lambda e, tmp=tmp, pw=pw, gate=gate: e.tensor_tensor(tmp.ap, pw.ap, gate.ap, ALU.mult), reads=[pw, gate], writes=[tmp])
                S.op("pool", lambda e, tmp=tmp, c=c: e.tensor_tensor(h.ap[:, c, :], h.ap[:, c, :], tmp.ap, ALU.add), reads=[(h, c), tmp], writes=[(h, c)])
                dfr.flush()
                r = after_chunk(c)
                if r is not None:
                    dfr.push(r)
            dfr.flush()
            ring.release(("wgate", ti, l))
            ring.release(("plein", ti, l))

        def attention(ti, xn, xn_next):
            ph = Phase()
            qT = ph.t("qT", BF16, [KC, T], nparts=KC)
            oT = ph.t("oT", BF16, [KC, T], nparts=NCH)
            NPT = 8
            pts = [ph.t(f"PT{i}", BF16, [512]) for i in range(NPT)]
            lnDs = [ph.t(f"lnD{i}", F32, [512]) for i in range(2)]
            Rs = [ph.t(f"R{i}", F32, [512]) for i in range(2)]
            SB = [0, 1, 2, 3]

            def ps_S(name, dtype, shape):
                return ps_from("S", SB, name, dtype, shape)

            wkv = ring.get(("wkv", ti))
            for j in range(2):
                pk = ps_narrow("pk", F32, [T])
                mm_group([(pk.ap, wkv.ap[:, k, j * 128:(j + 1) * 128], xn.ap[:, k, :], k == 0, k == KC - 1) for k in range(KC)],
                         reads=[wkv, xn], writes=[pk])
                S.op("act", lambda e, pk=pk, j=j: e.activation(kT.ap[:, j, 128:128 + T], pk.ap, AF.Copy), reads=[pk],
                     writes=[(kT, list(range(1, NCH + 1)))])
            for c in range(NCH):
                pv = ps_narrow("pvt", F32, [256])
                mm_group([(pv.ap, xn.ap[:, k, c * 128:(c + 1) * 128], wkv.ap[:, k, 256:512], k == 0, k == KC - 1) for k in range(KC)],
                         reads=[(xn, c), wkv], writes=[pv])
                pv4 = pv.ap.rearrange("p (j gi d) -> p j gi d", gi=2, d=64)
                vp4 = vpad.ap[:, 1 + c, :, :].rearrange("p (j gi) x -> p j gi x", gi=2)
                S.op("dve", lambda e, pv4=pv4, vp4=vp4: e.tensor_copy(vp4[:, :, 0, 0:64], pv4[:, :, 0, :]), reads=[pv], writes=[(vpad, 1 + c)])
                S.op("dve", lambda e, pv4=pv4, vp4=vp4: e.tensor_copy(vp4[:, :, 1, 64:128], pv4[:, :, 1, :]), reads=[pv], writes=[(vpad, 1 + c)])
            ring.release(("wkv", ti))
            for j in range(4):
                wq = ring.get(("wq", ti, j))
                for cc in range(2):
                    m = 2 * j + cc
                    pq = ps_narrow("pq", F32, [T])
                    mm_group([(pq.ap, wq.ap[:, k, cc * 128:(cc + 1) * 128], xn.ap[:, k, :], k == 0, k == KC - 1) for k in range(KC)],
                             reads=[wq, xn], writes=[pq])
                    S.op("act", lambda e, pq=pq, m=m: e.activation(qT.ap[:, m, :], pq.ap, AF.Copy), reads=[pq], writes=[(qT, m)])
                ring.release(("wq", ti, j))
            wo = ring.get(("wo", ti))
            cnt = {"e": 0, "r": 0}

            def unit_scores(c, jp, gi):
                g = 2 * jp + gi
                pb_ = gi * 64
                nb = ti * NCH + c
                whichs = [0, 1] if nb > 0 else [1]
                outl = []
                for which in whichs:
                    blk = c + which
                    pS = ps_S("pS", F32, [512])
                    mm_group([(pS.ap, kT.ap[pb_:pb_ + 64, jp, blk * 128:(blk + 1) * 128],
                               qT.ap[pb_:pb_ + 64, jp * 4:(jp + 1) * 4, c * 128:(c + 1) * 128], True, True)],
                             reads=[(kT, blk), qT], writes=[pS])
                    pt_ = pts[cnt["e"] % NPT]
                    cnt["e"] += 1
                    S.op("act", lambda e, pt_=pt_, pS=pS: e.activation(pt_.ap, pS.ap, AF.Exp, scale=0.125), reads=[pS], writes=[pt_])
                    S.op("dve" if which == 1 else "pool",
                         lambda e, pt_=pt_, which=which, g=g: e.tensor_tensor(pt_.ap, pt_.ap, Mk.ap[:, which, g, :], ALU.mult),
                         reads=[pt_, Mk], writes=[pt_])
                    outl.append((pt_, blk, which))
                return outl

            def unit_pv(c, jp, gi, pl, O, Dn):
                g = 2 * jp + gi
                specs = []
                reads = [ones_eo, vpad]
                first = gi == 0
                for wi, (pt_, blk, which) in enumerate(pl):
                    st_ = first and wi == 0
                    specs.append((O.ap, vpad.ap[:, blk, g, :], pt_.ap, st_, gi == 1 and wi == len(pl) - 1))
                    specs.append((Dn.ap, ones_eo.ap[:, gi, :], pt_.ap, st_, False))
                    reads += [pt_]
                if gi == 1:
                    o_, l_, r_, st_, _ = specs[-1]
                    specs[-1] = (o_, l_, r_, st_, True)
                mm_group(specs, reads=reads, writes=[O, Dn])

            def finish_pair(c, jp, O, Dn):
                lnD = lnDs[cnt["r"] % 2]
                R = Rs[cnt["r"] % 2]
                cnt["r"] += 1
                S.op("dve", lambda e: e.tensor_tensor(lnD.ap, Dn.ap, sinkt.ap[:, jp, :], ALU.add), reads=[Dn, sinkt], writes=[lnD])
                S.op("act", lambda e: e.activation(lnD.ap, lnD.ap, AF.Ln), reads=[lnD], writes=[lnD])
                S.op("act", lambda e: e.activation(R.ap, lnD.ap, AF.Exp, scale=-1.0), reads=[lnD], writes=[R])
                S.op("dve", lambda e: e.tensor_tensor(oT.ap[:, jp * 4:(jp + 1) * 4, c * 128:(c + 1) * 128],
                                                      O.ap.rearrange("p (r i) -> p r i", r=4), R.ap.rearrange("p (r i) -> p r i", r=4), ALU.mult),
                     reads=[O, R], writes=[(oT, c)])

            dfr = Deferred()

            def emit_wo(c):
                psr["n"] += 1
                po = M.ps(f"pwo_{psr['n']}", 0, F32, [D], nbanks=2)
                mm_group([(po.ap[:, hf * 512:(hf + 1) * 512], oT.ap[:, k, c * 128:(c + 1) * 128], wo.ap[:, k, hf * 512:(hf + 1) * 512], k == 0, k == KC - 1)
                          for hf in range(2) for k in range(KC)], reads=[(oT, c), wo], writes=[po])
                S.op("dve", lambda e: e.tensor_tensor(h.ap[:, c, :], po.ap, h.ap[:, c, :], ALU.add), reads=[po, (h, c)], writes=[(h, c)])
                xt = norm_a(c, stats[4][c])
                dfr.push(lambda: norm_b(c, xn_next, xt, tp_alloc=ps_S))

            units = [(c, jp, gi) for c in range(NCH) for jp in range(2) for gi in range(2)]
            SK = 1
            N = len(units)
            pls = {}
            wo_q = []
            fin_q = []
            OD = {}
            for idx in range(N + SK + 3):
                if idx < N:
                    pls[idx] = unit_scores(*units[idx])
                dfr.flush()
                for cq in wo_q:
                    emit_wo(cq)
                wo_q = []
                for (c_, jp_, O_, Dn_) in fin_q:
                    finish_pair(c_, jp_, O_, Dn_)
                    if jp_ == 1:
                        wo_q.append(c_)
                fin_q = []
                j = idx - SK
                if 0 <= j < N:
                    c, jp, gi = units[j]
                    if gi == 0:
                        psr["n"] += 1
                        b0 = 4 + 2 * ((2 * c + jp) % 2)
                        OD[(c, jp)] = (M.ps(f"O_{psr['n']}", b0, F32, [512]), M.ps(f"Dn_{psr['n']}", b0 + 1, F32, [512]))
                    O, Dn = OD[(c, jp)]
                    unit_pv(c, jp, gi, pls.pop(j), O, Dn)
                    if gi == 1:
                        fin_q.append((c, jp, O, Dn))
            assert not wo_q and not fin_q
            dfr.flush()
            ring.release(("wo", ti))
            S.op("pool", lambda e: e.tensor_copy(kT.ap[:, :, 0:128], kT.ap[:, :, T:T + 128]), reads=[(kT, NCH)], writes=[(kT, 0)])
            S.op("pool", lambda e: e.tensor_copy(vpad.ap[:, 0, :, :], vpad.ap[:, NCH, :, :]), reads=[(vpad, NCH)], writes=[(vpad, 0)])
        return dict(norm_chunk=norm_chunk, norm_a=norm_a, norm_b=norm_b, final_chunk=final_chunk, load_p=load_p,
                    mixer_a=mixer_a, ffn=ffn, ple=ple, attention=attention)

    PH = [make_phases(hA), make_phases(hB)]
    for c in range(NCH):
        S.dma("sp", hA.ap[:, c, :], x_d[c * 128:(c + 1) * 128, :], writes=[(hA, c)])
    ring.prefetch()
    for c in range(NCH):
        PH[0]["norm_chunk"](c, xn_a, stats[0][c])
    for ti in range(NT):
        P = PH[ti % 2]
        Pn = PH[(ti + 1) % 2]
        hn = hs[(ti + 1) % 2]
        if ti + 1 < NT:
            for c in range(NCH):
                r1 = (ti + 1) * T + c * 128
                S.dma("sp", hn.ap[:, c, :], x_d[r1:r1 + 128, :], writes=[(hn, c)])
        if ti == 0:
            P["load_p"](0, (0, 1))
        P["mixer_a"](ti, xn_a, xn_b)
        if ti > 0:
            P["load_p"](ti, (1,))
        P["ffn"](ti, 0, xn_b, xn_a, 2)

        def after0(c, P=P):
            xt = P["norm_a"](c, stats[3][c])
            return lambda: P["norm_b"](c, xn_b, xt)
        P["ple"](ti, 0, xn_a, after0)
        P["attention"](ti, xn_b, xn_a)
        P["ffn"](ti, 1, xn_a, xn_b, 5)
        if ti + 1 < NT:
            P["load_p"](ti + 1, (0,))
            for c in range(NCH):
                Pn["norm_chunk"](c, xn_a, stats[0][c])
        phf = Phase()
        phf.b.take(16 * 1024)
        ots = [phf.t(f"ot{i}", F32, [D]) for i in range(2)]
        P["ple"](ti, 1, xn_b, lambda c, ti=ti, P=P, ots=ots: P["final_chunk"](ti, c, stats[6][c], ots[c % 2]))
    S.final_wait("pool", [out_t])
    S.emit()
    return nc, S


_CACHE = {}


def kernel(**inputs):
    x = np.asarray(inputs["x"], dtype=np.float32)
    p = np.asarray(inputs["p"], dtype=np.float32)
    B, S_len, _ = x.shape
    key = (S_len,)
    if key not in _CACHE:
        _CACHE[key] = build_program(S_len)[0]
    nc = _CACHE[key]
    ident = np.eye(128, dtype=np.float32)
    shared = {k: np.ascontiguousarray(np.asarray(inputs[k], dtype=np.float32)) for k in INPUT_SHAPES}
    in_maps = []
    for b in range(B):
        m = dict(shared)
        m["x"] = np.ascontiguousarray(x[b])
        m["p"] = np.ascontiguousarray(p[:, b])
        m["ident"] = ident
        in_maps.append(m)
    res = run_bass_kernel_spmd(nc, in_maps, core_ids=list(range(B)))
    return np.stack([np.asarray(r["out"], dtype=np.float32) for r in res.results], axis=0)
```
